# Optimizing a Trainium2 kernel written in Bass

```python
import jax, jax.numpy as jnp
from jax import lax
import numpy as np


D_MODEL = 1024
BATCH = 8
SEQ = 4096
DEPTH = 2

CTX_LEN = 256
GRID_W = 64
N_BRANCH = 4
BR_W = D_MODEL // N_BRANCH
HEAD_DIM = 64
HEADS = BR_W // HEAD_DIM
N_PARTS = 12
P_IN = N_PARTS * BR_W
CHUNK = 64
GLA_LR = 16
GLA_NORMALIZER = 16.0
RWKV_DECAY_LR = 64
RWKV_A_LR = 64
RWKV_G_LR = 160
D_FF = 2816
ROPE_BASE = 10000.0
EPS = 1e-6

kernel_name = 'hybrid_gla_retnet_rwkv7_fnet_prefix_block'


def _rmsnorm(x, g):
    xf = x.astype(jnp.float32)
    y = xf * lax.rsqrt(jnp.mean(xf * xf, axis=-1, keepdims=True) + EPS)
    return y.astype(x.dtype) * g


def _head_norm(o, g, center):
    of = o.astype(jnp.float32)
    if center:
        of = of - jnp.mean(of, axis=-1, keepdims=True)
    of = of * lax.rsqrt(jnp.mean(of * of, axis=-1, keepdims=True) + EPS)
    B, T, H, d = o.shape
    return of.reshape(B, T, H * d) * g


def _heads(x):
    B, T, _ = x.shape
    return x.reshape(B, T, HEADS, HEAD_DIM).transpose(0, 2, 1, 3)


def _unheads(x):
    return x.transpose(0, 2, 1, 3)


def _to_chunks(x):
    B, H, T, d = x.shape
    return x.reshape(B, H, T // CHUNK, CHUNK, d).transpose(2, 0, 1, 3, 4)


def _from_chunks(x):
    N, B, H, C, d = x.shape
    return x.transpose(1, 2, 0, 3, 4).reshape(B, H, N * C, d)


def _neighbours(x):
    xp = jnp.pad(x, ((0, 0), (1, 1), (0, 0)))
    return xp[:, :-2], xp[:, 2:]


def _dwconv3(x, w):
    prev, nxt = _neighbours(x)
    return prev * w[0] + x * w[1] + nxt * w[2]


def _axial_rope(T):
    rows = T // GRID_W
    row = jnp.repeat(jnp.arange(rows, dtype=jnp.float32), GRID_W)
    col = jnp.tile(jnp.arange(GRID_W, dtype=jnp.float32), rows)
    n_freq = HEAD_DIM // 4
    inv = ROPE_BASE ** (-jnp.arange(n_freq, dtype=jnp.float32) / n_freq)
    ang = jnp.concatenate([row[:, None] * inv, col[:, None] * inv], axis=-1)
    return jnp.cos(ang), jnp.sin(ang)


def _rope(x, cos, sin):
    half = x.shape[-1] // 2
    x1, x2 = x[..., :half], x[..., half:]
    return jnp.concatenate([x1 * cos - x2 * sin, x1 * sin + x2 * cos], axis=-1)


def _gla_scan(q, k, v, log_a, s0, reverse):
    if s0 is None:
        s0 = jnp.zeros(q.shape[:2] + (q.shape[-1], v.shape[-1]), jnp.float32)
    if reverse:
        q, k, v, log_a = (jnp.flip(t, axis=2) for t in (q, k, v, log_a))
    tri = jnp.tril(jnp.ones((CHUNK, CHUNK), dtype=bool))[:, :, None]

    def step(s, inp):
        qc, kc, vc, ac = inp
        b = jnp.cumsum(ac, axis=2)
        diff = b[:, :, :, None, :] - b[:, :, None, :, :]
        decay = jnp.where(tri, jnp.exp(jnp.where(tri, diff, 0.0)), 0.0)
        scores = jnp.einsum('bhid,bhjd,bhijd->bhij', qc, kc, decay)
        o = (jnp.einsum('bhij,bhje->bhie', scores, vc)
             + jnp.einsum('bhid,bhde->bhie', qc * jnp.exp(b), s))
        b_end = b[:, :, -1:, :]
        s = (jnp.swapaxes(jnp.exp(b_end), 2, 3) * s
             + jnp.einsum('bhjd,bhje->bhde', kc * jnp.exp(b_end - b), vc))
        return s, o

    s, o = lax.scan(step, s0, tuple(_to_chunks(t) for t in (q, k, v, log_a)))
    o = _from_chunks(o)
    if reverse:
        o = jnp.flip(o, axis=2)
    return o, s


def _ret_log_decay(direction):
    expo = -5.0 - jnp.arange(HEADS, dtype=jnp.float32)
    if direction == 1:
        expo = expo[::-1]
    return jnp.log1p(-jnp.exp2(expo))


def _ret_scan(q, k, v, log_g, s0, reverse):
    if s0 is None:
        s0 = jnp.zeros(q.shape[:2] + (q.shape[-1], v.shape[-1]), jnp.float32)
    if reverse:
        q, k, v = (jnp.flip(t, axis=2) for t in (q, k, v))
    pos = jnp.arange(CHUNK, dtype=jnp.float32)
    rel = pos[:, None] - pos[None, :]
    tri = rel >= 0
    dmat = jnp.where(tri, jnp.exp(jnp.where(tri, rel, 0.0) * log_g[:, None, None]), 0.0)
    q_dec = jnp.exp((pos + 1.0) * log_g[:, None])[:, :, None]
    k_dec = jnp.exp((CHUNK - 1.0 - pos) * log_g[:, None])[:, :, None]
    c_dec = jnp.exp(CHUNK * log_g)[:, None, None]

    def step(s, inp):
        qc, kc, vc = inp
        scores = jnp.einsum('bhid,bhjd->bhij', qc, kc) * dmat
        o = (jnp.einsum('bhij,bhje->bhie', scores, vc)
             + jnp.einsum('bhid,bhde->bhie', qc * q_dec, s))
        s = c_dec * s + jnp.einsum('bhjd,bhje->bhde', kc * k_dec, vc)
        return s, o

    s, o = lax.scan(step, s0, tuple(_to_chunks(t) for t in (q, k, v)))
    o = _from_chunks(o)
    if reverse:
        o = jnp.flip(o, axis=2)
    return o, s


def _rwkv_scan(r, w, k, v, a, b, s0, reverse):
    if s0 is None:
        B, T, H, d = r.shape
        s0 = jnp.zeros((B, H, d, d), jnp.float32)
    xs = tuple(jnp.moveaxis(t, 1, 0) for t in (r, w, k, v, a, b))
    if reverse:
        xs = tuple(jnp.flip(t, axis=0) for t in xs)

    def step(s, inp):
        rt, wt, kt, vt, at, bt = inp
        sa = jnp.einsum('bhvk,bhk->bhv', s, at)
        s = (s * wt[:, :, None, :] + sa[..., None] * bt[:, :, None, :]
             + vt[..., None] * kt[:, :, None, :])
        return s, jnp.einsum('bhvk,bhk->bhv', s, rt)

    s, ys = lax.scan(step, s0, xs)
    if reverse:
        ys = jnp.flip(ys, axis=0)
    return jnp.moveaxis(ys, 0, 1), s


def _bidir(scan_fn, lat_dirs, ctx_dirs):
    out_l = 0.0
    out_c = 0.0
    for dr in range(2):
        oc, sc = scan_fn(*ctx_dirs[dr], None, dr == 1)
        ol, _ = scan_fn(*lat_dirs[dr], sc, dr == 1)
        out_l = out_l + ol
        out_c = out_c + oc
    return out_l, out_c


def _gla_feats(h, q, k, v, lp):
    f32 = jnp.float32
    q = _heads(q).astype(f32) * HEAD_DIM ** -0.5
    k = _heads(k).astype(f32)
    v = _heads(v).astype(f32)
    dirs = []
    for dr in range(2):
        z = (h @ lp['gla_wa1'][dr]) @ lp['gla_wa2'][dr] + lp['gla_ba'][dr]
        log_a = _heads(jax.nn.log_sigmoid(z.astype(f32)) / GLA_NORMALIZER)
        dirs.append((q, k, v, log_a))
    return dirs


def _ret_feats(q, k, v, rope):
    f32 = jnp.float32
    q = _heads(q).astype(f32)
    k = _heads(k).astype(f32) * HEAD_DIM ** -0.5
    v = _heads(v).astype(f32)
    if rope is not None:
        q = _rope(q, *rope)
        k = _rope(k, *rope)
    return [(q, k, v, _ret_log_decay(dr)) for dr in range(2)]


def _rwkv_feats(h, r, k, v, lp):
    f32 = jnp.float32
    B, T, _ = h.shape
    shp = (B, T, HEADS, HEAD_DIM)
    r, k, v = jnp.split(_dwconv3(jnp.concatenate([r, k, v], axis=-1), lp['rwkv_conv']), 3, axis=-1)
    prev, nxt = _neighbours(h)
    xx = 0.5 * (prev + nxt) - h
    xw = h + xx * lp['rwkv_mu'][0]
    xa = h + xx * lp['rwkv_mu'][1]
    xg = h + xx * lp['rwkv_mu'][2]
    g = jax.nn.sigmoid(xg @ lp['rwkv_g1']) @ lp['rwkv_g2']
    kk = (k * lp['rwkv_kk']).astype(f32).reshape(shp)
    kk = kk * lax.rsqrt(jnp.sum(kk * kk, axis=-1, keepdims=True) + EPS)
    r4 = r.astype(f32).reshape(shp)
    v4 = v.astype(f32).reshape(shp)
    rk = lp['rwkv_rk'].reshape(HEADS, HEAD_DIM)
    dirs = []
    bonus = 0.0
    for dr in range(2):
        w_raw = -jax.nn.softplus(-(lp['rwkv_w0'][dr] + jnp.tanh(xw @ lp['rwkv_w1'][dr]) @ lp['rwkv_w2'][dr])) - 0.5
        decay = jnp.exp(-jnp.exp(w_raw.astype(f32))).reshape(shp)
        a = jax.nn.sigmoid(lp['rwkv_a0'][dr] + (xa @ lp['rwkv_a1'][dr]) @ lp['rwkv_a2'][dr]).astype(f32)
        kd = (k.astype(f32) * (1.0 + (a - 1.0) * lp['rwkv_ka'])).reshape(shp)
        a4 = a.reshape(shp)
        dirs.append((r4, decay, kd, v4, -kk, kk * a4))
        bonus = bonus + jnp.sum(r4 * kd * rk, axis=-1, keepdims=True) * v4
    return dirs, bonus, g


def _fourier(f):
    B, T, _ = f.shape
    fg = f.astype(jnp.float32).reshape(B, T, HEADS, HEAD_DIM)
    return jnp.real(jnp.fft.fft2(fg, axes=(1, 3), norm='ortho')).reshape(B, T, BR_W)


def _branch_outputs(parts, o_gla, o_ret, y_rwkv, bonus, g_rwkv, lp):
    f32 = jnp.float32
    dt = parts[0].dtype
    B, T, _ = parts[0].shape
    gla = _head_norm(_unheads(o_gla), lp['gla_gn'], False) * jax.nn.silu(parts[3].astype(f32))
    ret = _head_norm(_unheads(o_ret), lp['ret_gn'], True) * jax.nn.silu(parts[7].astype(f32))
    rwkv = (_head_norm(y_rwkv, lp['rwkv_gn'], True) + bonus.reshape(B, T, BR_W)) * g_rwkv.astype(f32)
    fnet = _fourier(parts[11])
    return [t.astype(dt) for t in (gla, ret, rwkv, fnet)]


def _merge(h, outs, lp):
    z = 0.0
    for i in range(N_BRANCH):
        gate = jax.nn.sigmoid(h @ lp['w_gate'][i] + lp['b_gate'][i])
        z = z + gate * (outs[i] @ lp['w_br'][i])
    return z @ lp['w_out']


def _hybrid_mixer(h, hc, lp, rope, need_ctx):
    pl = jnp.split(h @ lp['w_in'], N_PARTS, axis=-1)
    pc = jnp.split(hc @ lp['w_in'], N_PARTS, axis=-1)
    o_gla, o_gla_c = _bidir(_gla_scan, _gla_feats(h, pl[0], pl[1], pl[2], lp),
                            _gla_feats(hc, pc[0], pc[1], pc[2], lp))
    o_ret, o_ret_c = _bidir(_ret_scan, _ret_feats(pl[4], pl[5], pl[6], rope),
                            _ret_feats(pc[4], pc[5], pc[6], None))
    rl_dirs, rl_bonus, rl_g = _rwkv_feats(h, pl[8], pl[9], pl[10], lp)
    rc_dirs, rc_bonus, rc_g = _rwkv_feats(hc, pc[8], pc[9], pc[10], lp)
    y_rwkv, y_rwkv_c = _bidir(_rwkv_scan, rl_dirs, rc_dirs)
    y = _merge(h, _branch_outputs(pl, o_gla, o_ret, y_rwkv, rl_bonus, rl_g, lp), lp)
    if not need_ctx:
        return y, None
    yc = _merge(hc, _branch_outputs(pc, o_gla_c, o_ret_c, y_rwkv_c, rc_bonus, rc_g, lp), lp)
    return y, yc


def _conv_ffn(h, lp):
    a, u = jnp.split(h @ lp['ffn_up'], 2, axis=-1)
    a = _dwconv3(a, lp['ffn_conv']) + lp['ffn_conv_b']
    return (jax.nn.silu(a) * u) @ lp['ffn_down']


def _modulation(cvec, w, b):
    return jnp.split(jax.nn.silu(cvec) @ w + b, 6, axis=-1)


def setup_inputs(seed: int = 0) -> dict:
    key = jax.random.key(seed)
    keys = jax.random.split(key, 40)
    L, D, F = DEPTH, D_MODEL, D_FF
    f32 = jnp.float32

    def nrm(i, shape, scale):
        return jax.random.normal(keys[i], shape, f32) * scale

    def gain(i, shape):
        return 1.0 + nrm(i, shape, 0.02)

    conv_base = jnp.array([0.25, 0.5, 0.25], f32)[None, :, None]
    return {
        'x': nrm(0, (BATCH, SEQ, D), 1.0),
        'c': nrm(1, (BATCH, D), 1.0),
        'ctx': nrm(2, (BATCH, CTX_LEN, D), 1.0),
        'c_ctx': nrm(3, (D,), 1.0),
        'w_ada': nrm(4, (L, D, 6 * D), 0.5 * D ** -0.5),
        'b_ada': nrm(5, (L, 6 * D), 0.01),
        'g_norm1': gain(6, (L, D)),
        'g_norm2': gain(7, (L, D)),
        'w_in': nrm(8, (L, D, P_IN), D ** -0.5),
        'gla_wa1': nrm(9, (L, 2, D, GLA_LR), D ** -0.5),
        'gla_wa2': nrm(10, (L, 2, GLA_LR, BR_W), GLA_LR ** -0.5),
        'gla_ba': nrm(11, (L, 2, BR_W), 0.5),
        'gla_gn': gain(12, (L, BR_W)),
        'ret_gn': gain(13, (L, BR_W)),
        'rwkv_conv': conv_base + nrm(14, (L, 3, 3 * BR_W), 0.05),
        'rwkv_mu': jax.random.uniform(keys[15], (L, 3, D), f32),
        'rwkv_w0': nrm(16, (L, 2, BR_W), 1.0),
        'rwkv_w1': nrm(17, (L, 2, D, RWKV_DECAY_LR), D ** -0.5),
        'rwkv_w2': nrm(18, (L, 2, RWKV_DECAY_LR, BR_W), 0.5 * RWKV_DECAY_LR ** -0.5),
        'rwkv_a0': nrm(19, (L, 2, BR_W), 0.5),
        'rwkv_a1': nrm(20, (L, 2, D, RWKV_A_LR), D ** -0.5),
        'rwkv_a2': nrm(21, (L, 2, RWKV_A_LR, BR_W), 0.5 * RWKV_A_LR ** -0.5),
        'rwkv_g1': nrm(22, (L, D, RWKV_G_LR), D ** -0.5),
        'rwkv_g2': nrm(23, (L, RWKV_G_LR, BR_W), RWKV_G_LR ** -0.5),
        'rwkv_kk': 0.85 + nrm(24, (L, BR_W), 0.05),
        'rwkv_ka': gain(25, (L, BR_W)),
        'rwkv_rk': nrm(26, (L, BR_W), 0.1),
        'rwkv_gn': gain(27, (L, BR_W)),
        'w_gate': nrm(28, (L, N_BRANCH, D, D), D ** -0.5),
        'b_gate': nrm(29, (L, N_BRANCH, D), 0.01),
        'w_br': nrm(30, (L, N_BRANCH, BR_W, D), BR_W ** -0.5),
        'w_out': nrm(31, (L, D, D), D ** -0.5),
        'ffn_up': nrm(32, (L, D, 2 * F), D ** -0.5),
        'ffn_conv': conv_base + nrm(33, (L, 3, F), 0.05),
        'ffn_conv_b': nrm(34, (L, F), 0.01),
        'ffn_down': nrm(35, (L, F, D), F ** -0.5),
        'g_final': gain(36, (D,)),
    }


def reference(x, c, ctx, c_ctx, w_ada, b_ada, g_norm1, g_norm2, w_in, gla_wa1, gla_wa2, gla_ba,
              gla_gn, ret_gn, rwkv_conv, rwkv_mu, rwkv_w0, rwkv_w1, rwkv_w2, rwkv_a0, rwkv_a1,
              rwkv_a2, rwkv_g1, rwkv_g2, rwkv_kk, rwkv_ka, rwkv_rk, rwkv_gn, w_gate, b_gate, w_br,
              w_out, ffn_up, ffn_conv, ffn_conv_b, ffn_down, g_final):
    rope = _axial_rope(x.shape[1])
    for l in range(DEPTH):
        last = l == DEPTH - 1
        lp = {
            'w_in': w_in[l], 'gla_wa1': gla_wa1[l], 'gla_wa2': gla_wa2[l], 'gla_ba': gla_ba[l],
            'gla_gn': gla_gn[l], 'ret_gn': ret_gn[l], 'rwkv_conv': rwkv_conv[l], 'rwkv_mu': rwkv_mu[l],
            'rwkv_w0': rwkv_w0[l], 'rwkv_w1': rwkv_w1[l], 'rwkv_w2': rwkv_w2[l],
            'rwkv_a0': rwkv_a0[l], 'rwkv_a1': rwkv_a1[l], 'rwkv_a2': rwkv_a2[l],
            'rwkv_g1': rwkv_g1[l], 'rwkv_g2': rwkv_g2[l], 'rwkv_kk': rwkv_kk[l],
            'rwkv_ka': rwkv_ka[l], 'rwkv_rk': rwkv_rk[l], 'rwkv_gn': rwkv_gn[l],
            'w_gate': w_gate[l], 'b_gate': b_gate[l], 'w_br': w_br[l], 'w_out': w_out[l],
            'ffn_up': ffn_up[l], 'ffn_conv': ffn_conv[l], 'ffn_conv_b': ffn_conv_b[l],
            'ffn_down': ffn_down[l],
        }
        m = [t[:, None, :] for t in _modulation(c, w_ada[l], b_ada[l])]
        mc = _modulation(c_ctx, w_ada[l], b_ada[l])
        h = _rmsnorm(x, g_norm1[l]) * (1.0 + m[1]) + m[0]
        hc = _rmsnorm(ctx, g_norm1[l]) * (1.0 + mc[1]) + mc[0]
        y, yc = _hybrid_mixer(h, hc, lp, rope, not last)
        x = x + m[2] * y
        x = x + m[5] * _conv_ffn(_rmsnorm(x, g_norm2[l]) * (1.0 + m[4]) + m[3], lp)
        if not last:
            ctx = ctx + mc[2] * yc
            ctx = ctx + mc[5] * _conv_ffn(_rmsnorm(ctx, g_norm2[l]) * (1.0 + mc[4]) + mc[3], lp)
    return _rmsnorm(x, g_final)
```

```python
import math
from contextlib import ExitStack
import numpy as np
import ml_dtypes
import concourse.bass as bass
import concourse.mybir as mybir
from concourse.bass_utils import run_bass_kernel_spmd

F32 = mybir.dt.float32
BF16 = mybir.dt.bfloat16
ALU = mybir.AluOpType
AF = mybir.ActivationFunctionType

D = 1024
TC = 256
NLAYER = 2
FF = 2816
EPS = 1e-6
COMPUTE = ("pe", "act", "dve", "pool")
QUEUES = ("sp", "pool")
EPOCH = 30000
RING = 6


class V:
    __slots__ = ("key", "ap")

    def __init__(self, key, ap):
        self.key = key
        self.ap = ap

    def __getitem__(self, idx):
        return V(self.key, self.ap[idx])

    def k(self, key):
        return V(key, self.ap)


class Sched:
    def __init__(self, nc, stack):
        self.nc = nc
        self.stack = stack
        self.q = {e: [] for e in ("pe", "act", "dve", "pool", "sp")}
        self.cnt = {e: 0 for e in COMPUTE}
        self.rank = {e: 0 for e in COMPUTE}
        self.targets = {e: set() for e in COMPUTE}
        self.esems = {e: [] for e in COMPUTE}
        self.seen_c = {e: {} for e in self.q}
        self.seen_d = {e: {} for e in self.q}
        self.bufs = {}
        self.ring = {}
        self.ringcnt = {}
        for qn in QUEUES:
            self.ring[qn] = [self._sem(f"dq_{qn}_{i}") for i in range(RING)]
            self.ringcnt[qn] = 0
        self.ninst = 0

    def _sem(self, name):
        return self.stack.enter_context(self.nc.semaphore(name))

    def _engine_sem(self, e, idx):
        while len(self.esems[e]) <= idx:
            self.esems[e].append(self._sem(f"e_{e}_{len(self.esems[e])}"))
        return self.esems[e][idx]

    def _need(self, eng, ref):
        if ref[0] == "c":
            _, src, idx = ref
            if self.seen_c[eng].get(src, 0) >= idx:
                return
            self.seen_c[eng][src] = idx
            self.targets[src].add(idx)
            self.q[eng].append(("waitc", src, idx))
        else:
            _, sem, val, _qn = ref
            key = id(sem)
            if self.seen_d[eng].get(key, 0) >= val:
                return
            self.seen_d[eng][key] = val
            self.q[eng].append(("waitd", sem, val))

    def _deps(self, reads, writes):
        deps = []
        for b in reads:
            st = self.bufs.get(b)
            if st and st[0] is not None:
                deps.append((st[0], True))
        for b in writes:
            st = self.bufs.get(b)
            if st:
                if st[0] is not None:
                    deps.append((st[0], False))
                for r in st[1]:
                    deps.append((r, False))
        return deps

    def _update(self, ref, reads, writes):
        for b in reads:
            st = self.bufs.setdefault(b, [None, []])
            if ref[0] == "c":
                st[1] = [r for r in st[1] if not (r[0] == "c" and r[1] == ref[1])]
            st[1].append(ref)
        for b in writes:
            self.bufs[b] = [ref, []]

    def op(self, eng, fn, reads=(), writes=()):
        for (d, raw) in self._deps(reads, writes):
            if d[0] == "c" and d[1] == eng and not raw:
                continue
            self._need(eng, d)
        self.cnt[eng] += 1
        idx = self.cnt[eng]
        self.q[eng].append(("op", fn, idx))
        self._update(("c", eng, idx), reads, writes)
        self.ninst += 1

    def dma(self, qn, out, in_, is_output=False, **kw):
        reads, writes = [in_.key], [out.key]
        for (d, raw) in self._deps(reads, writes):
            self._need(qn, d)
        i = self.ringcnt[qn]
        self.ringcnt[qn] += 1
        sem = self.ring[qn][i % RING]
        prev = (i // RING) * 16
        if prev > 0:
            self._need(qn, ("d", sem, prev, qn))
        val = prev + 16
        oa, ia = out.ap, in_.ap

        def fn(e, oa=oa, ia=ia, kw=kw):
            return e.dma_start(out=oa, in_=ia, **kw)

        self.q[qn].append(("dma", fn, sem))
        self._update(("d", sem, val, qn), reads, writes)
        self.ninst += 1

    def barrier(self):
        refs = []
        for e in COMPUTE:
            if self.cnt[e] > 0:
                refs.append(("c", e, self.cnt[e]))
        for qn in QUEUES:
            n = self.ringcnt[qn]
            for j in range(min(RING, n)):
                cntj = (n - 1 - j) // RING + 1
                refs.append(("d", self.ring[qn][j], cntj * 16, qn))
        for e in self.q:
            for ref in refs:
                self._need(e, ref)
        self.bufs = {}

    def emit_block(self):
        nc = self.nc
        q = self.q
        self.q = {e: [] for e in q}
        semval = {}
        for e in COMPUTE:
            for idx in sorted(self.targets[e]):
                self.rank[e] += 1
                r = self.rank[e]
                semval[(e, idx)] = (self._engine_sem(e, (r - 1) // EPOCH), (r - 1) % EPOCH + 1)
            self.targets[e] = set()
        with nc.Block() as block:
            def mk(name):
                items = q[name]

                def body(e):
                    for item in items:
                        kind = item[0]
                        if kind == "waitc":
                            sem, val = semval[(item[1], item[2])]
                            e.wait_ge(sem, val)
                        elif kind == "waitd":
                            e.wait_ge(item[1], item[2])
                        elif kind == "dma":
                            item[1](e).then_inc(item[2], 16)
                        else:
                            inst = item[1](e)
                            sv = semval.get((name, item[2]))
                            if sv is not None:
                                inst.then_inc(sv[0], 1)
                return body
            block.tensor(mk("pe"))
            block.scalar(mk("act"))
            block.vector(mk("dve"))
            block.gpsimd(mk("pool"))
            block.sync(mk("sp"))


class KB:
    def __init__(self, nc, S):
        self.nc = nc
        self.S = S
        self.pstack = None
        self.uid = 0

    def xbegin(self):
        self.xstack = ExitStack()
        self.xstack.__enter__()

    def xend(self):
        self.xstack.__exit__(None, None, None)
        self.xstack = None

    def sb(self, name, shape, dt, persistent=False, x=False):
        st = self.S.stack if persistent else (self.xstack if x else self.pstack)
        self.uid += 1
        t = st.enter_context(self.nc.sbuf_tensor(f"{name}_{self.uid}", list(shape), dt))
        return V(f"{name}_{self.uid}", t[:])

    def ps(self, name, shape, dt=F32):
        self.uid += 1
        esz = 4 if dt == F32 else 2
        n = 1
        for d_ in shape[1:]:
            n *= d_
        nbanks = (n * esz + 2047) // 2048
        t = self.pstack.enter_context(self.nc.psum_tensor(f"{name}_{self.uid}", [128, 512 * nbanks], F32))
        ap = t[:]
        if dt != F32:
            ap = ap.bitcast(dt)
        ap = ap[0:shape[0], 0:n]
        if len(shape) == 3:
            ap = ap.rearrange("p (a b) -> p a b", a=shape[1])
        return V(f"{name}_{self.uid}", ap)

    def begin_phase(self):
        self.pstack = ExitStack()
        self.pstack.__enter__()

    def end_phase(self):
        self.S.barrier()
        self.S.emit_block()
        self.pstack.__exit__(None, None, None)
        self.pstack = None

    def _rw(self, outs, ins):
        return [v.key for v in ins if isinstance(v, V)], [v.key for v in outs]

    def mm(self, out, lhsT, rhs, start=True, stop=True, tp=None):
        r, w = self._rw([out], [lhsT, rhs])
        oa, la, ra = out.ap, lhsT.ap, rhs.ap
        if tp is None:
            self.S.op("pe", lambda e: e.matmul(oa, la, ra, start=start, stop=stop), r, w)
        else:
            self.S.op("pe", lambda e: e.matmul(oa, la, ra, start=start, stop=stop, tile_position=tp), r, w)

    def tr(self, out, in_, ident):
        r, w = self._rw([out], [in_, ident])
        oa, ia, da = out.ap, in_.ap, ident.ap
        self.S.op("pe", lambda e: e.transpose(oa, ia, da), r, w)

    def act(self, out, in_, func, bias=None, scale=None, accum=None, eng="act"):
        ins = [in_]
        kw = {}
        if bias is not None:
            if isinstance(bias, V):
                ins.append(bias)
                kw["bias"] = bias.ap
            else:
                kw["bias"] = float(bias)
        if scale is not None:
            if isinstance(scale, V):
                ins.append(scale)
                kw["scale"] = scale.ap
            else:
                kw["scale"] = float(scale)
        outs = [out]
        if accum is not None:
            outs.append(accum)
            kw["accum_out"] = accum.ap
        r, w = self._rw(outs, ins)
        oa, ia = out.ap, in_.ap
        self.S.op("act", lambda e: e.activation(oa, ia, func, **kw), r, w)

    def tt(self, eng, out, in0, in1, op):
        r, w = self._rw([out], [in0, in1])
        oa, a0, a1 = out.ap, in0.ap, in1.ap
        self.S.op(eng, lambda e: e.tensor_tensor(oa, a0, a1, op), r, w)

    def ts(self, eng, out, in0, s1, s2=None, op0=ALU.mult, op1=None):
        ins = [in0]
        a1 = s1
        a2 = s2
        if isinstance(s1, V):
            ins.append(s1)
            a1 = s1.ap
        if isinstance(s2, V):
            ins.append(s2)
            a2 = s2.ap
        r, w = self._rw([out], ins)
        oa, ia = out.ap, in0.ap
        if op1 is None:
            self.S.op(eng, lambda e: e.tensor_scalar(oa, ia, a1, None, op0), r, w)
        else:
            self.S.op(eng, lambda e: e.tensor_scalar(oa, ia, a1, a2, op0, op1), r, w)

    def stt(self, eng, out, in0, scalar, in1, op0, op1):
        ins = [in0, in1]
        sa = scalar
        if isinstance(scalar, V):
            ins.append(scalar)
            sa = scalar.ap
        r, w = self._rw([out], ins)
        oa, a0, a1 = out.ap, in0.ap, in1.ap
        eng = "dve"
        self.S.op(eng, lambda e: e.scalar_tensor_tensor(oa, a0, sa, a1, op0, op1), r, w)

    def recip(self, out, in_):
        r, w = self._rw([out], [in_])
        oa, ia = out.ap, in_.ap
        self.S.op("dve", lambda e: e.reciprocal(oa, ia), r, w)

    def copy(self, eng, out, in_):
        if eng == "act":
            return self.act(out, in_, AF.Copy)
        r, w = self._rw([out], [in_])
        oa, ia = out.ap, in_.ap
        self.S.op(eng, lambda e: e.tensor_copy(oa, ia), r, w)

    def memset(self, eng, out, val):
        r, w = self._rw([out], [])
        oa = out.ap
        self.S.op(eng, lambda e: e.memset(oa, val), r, w)

    def dma(self, out, in_, q="sp", **kw):
        self.S.dma(q, out, in_, **kw)


def bc(v, shape, axis):
    return V(v.key, v.ap.unsqueeze(axis).to_broadcast(list(shape)))


class Cfg:
    def __init__(self, TL=4096, nlayer=NLAYER, stop_after=None, debug=()):
        self.TL = TL
        self.NTOK = TC + TL
        self.NCH = self.NTOK // 128
        self.NCOL = self.NTOK + 4
        self.nlayer = nlayer
        self.stop_after = stop_after
        self.debug = tuple(debug)

    def col(self, tok):
        return tok + 1 if tok < TC else tok + 3

    def tiles(self):
        out = [(0, 0, TC)]
        for s in range(0, self.TL, 512):
            out.append((1, TC + s, min(512, self.TL - s)))
        return out


WEIGHT_NAMES = ["w_ada", "w_in", "gla_wa1", "gla_wa2", "rwkv_w1", "rwkv_w2", "rwkv_a1", "rwkv_a2",
                "rwkv_g1", "rwkv_g2", "w_gate", "w_br", "w_out", "ffn_up", "ffn_down"]
WEIGHT_SHAPES = {
    "w_ada": [NLAYER, D, 6 * D], "w_in": [NLAYER, D, 3072], "gla_wa1": [NLAYER, 2, D, 16],
    "gla_wa2": [NLAYER, 2, 16, 256], "rwkv_w1": [NLAYER, 2, D, 64], "rwkv_w2": [NLAYER, 2, 64, 256],
    "rwkv_a1": [NLAYER, 2, D, 64], "rwkv_a2": [NLAYER, 2, 64, 256], "rwkv_g1": [NLAYER, D, 160],
    "rwkv_g2": [NLAYER, 160, 256], "w_gate": [NLAYER, 4, D, D], "w_br": [NLAYER, 4, 256, D],
    "w_out": [NLAYER, D, D], "ffn_up": [NLAYER, D, 2 * FF], "ffn_down": [NLAYER, FF, D],
}

PV = {}
_o = 0
for _n, _c in [("b_ada", 48), ("g1", 8), ("g2", 8), ("mu", 24), ("gla_gn", 2), ("ret_gn", 2), ("rconv", 18),
               ("a0", 4), ("kkw", 2), ("ka", 2), ("rk", 2), ("rwkv_gn", 2), ("b_gate", 32), ("fconv", 66),
               ("fconvb", 22)]:
    PV[_n] = (_o, _c)
    _o += _c
NPV = _o


def build(cfg):
    nc = bass.Bass("TRN2", target_bir_lowering=False)
    TL, NTOK, NCH, NCOL = cfg.TL, cfg.NTOK, cfg.NCH, cfg.NCOL
    L = cfg.nlayer

    def din(name, shape, dt=F32):
        return V(name, nc.dram_tensor(name, list(shape), dt, kind="ExternalInput").ap())

    def dscr(name, shape, dt=F32):
        kind = "ExternalOutput" if name in cfg.debug else "Internal"
        return V(name, nc.dram_tensor(name, list(shape), dt, kind=kind).ap())

    x_in = din("x", [TL, D])
    ctx_in = din("ctx", [TC, D])
    cT_in = din("cT", [128, 8, 2])
    pv_in = din("pv", [NLAYER, 128, NPV])
    tmv_in = din("tmv", [NLAYER, 128, 4, 256])
    gfin_in = din("gfin", [128, D])
    cm_in = din("cmask", [128, 10, 128])
    rope_in = din("rope", [128, 2, TL])
    retla_in = din("retla", [128, 2, 256])
    dftL_in = din("dftL", [2, TL, TL], BF16)
    dftC_in = din("dftC", [2, TC, TC], BF16)
    dft64_in = din("dft64", [128, 2, 128])
    W = {n: din(n, WEIGHT_SHAPES[n]) for n in WEIGHT_NAMES}
    out_d = V("out", nc.dram_tensor("out", [TL, D], F32, kind="ExternalOutput").ap())

    xres = dscr("xres", [NTOK, D])
    hT_d = dscr("hT", [128, 8, NCOL], BF16)
    h2T_d = dscr("h2T", [128, 8, NCOL], BF16)
    A_qT = dscr("A_qT", [128, 4, NTOK], BF16)
    A_kT = dscr("A_kT", [128, 4, NTOK], BF16)
    A_gT = dscr("A_gT", [128, 4, NTOK], BF16)
    A_k = dscr("A_k", [NTOK, 512], BF16)
    A_v = dscr("A_v", [NTOK, 512], BF16)
    A_la = [dscr(f"A_la{d}", [NTOK, 256]) for d in range(2)]
    B_rT = dscr("B_rT", [128, 2, NTOK], BF16)
    B_aT = dscr("B_aT", [128, 2, NTOK], BF16)
    B_bT = [dscr(f"B_bT{d}", [128, 2, NTOK], BF16) for d in range(2)]
    B_kdT = [dscr(f"B_kdT{d}", [128, 2, NTOK], BF16) for d in range(2)]
    B_gT = dscr("B_gT", [128, 2, NTOK], BF16)
    B_bonT = dscr("B_bonT", [128, 2, NTOK], BF16)
    B_v = dscr("B_v", [NTOK, 256], BF16)
    B_b = [dscr(f"B_b{d}", [NTOK, 256], BF16) for d in range(2)]
    B_kd = [dscr(f"B_kd{d}", [NTOK, 256], BF16) for d in range(2)]
    B_lw = [dscr(f"B_lw{d}", [NTOK, 256]) for d in range(2)]
    F_fT = dscr("F_fT", [128, 2, NTOK], BF16)
    BR = dscr("BR", [128, 8, NTOK], BF16)

    with ExitStack() as top:
        S = Sched(nc, top)
        K = KB(nc, S)

        cm = K.sb("cm", [128, 10, 128], F32, True)
        cmb = K.sb("cmb", [128, 10, 128], BF16, True)
        pv = K.sb("pv", [128, NLAYER, NPV], F32, True)
        mod = K.sb("mod", [128, NLAYER, 48, 2], F32, True)
        sc1 = K.sb("sc1", [128, NLAYER, 8, 2], F32, True)
        sc2 = K.sb("sc2", [128, NLAYER, 8, 2], F32, True)
        omka = K.sb("omka", [128, NLAYER, 2], F32, True)
        epsc = K.sb("epsc", [128, 1], F32, True)
        IDENT, ONESBD, LE, LT, GE, GT, BLE, BLT, BGE, BGT = range(10)

        def pvc(l, name, i=0, n=1):
            o, c = PV[name]
            return pv[:, l, o + i:o + i + n]

        K.begin_phase()
        K.dma(cm, cm_in)
        K.dma(cmb, cm_in, q="pool")
        for l in range(L):
            K.dma(pv[:, l, :], pv_in[l])
        K.memset("dve", epsc, EPS)
        cT = K.sb("cT", [128, 8, 2], F32)
        scT = K.sb("scT", [128, 8, 2], F32)
        K.dma(cT, cT_in)
        K.act(scT, cT, AF.Silu)
        wblk = [K.sb(f"wblk{i}", [128, 8, 512], F32) for i in range(2)]
        psm = [K.ps(f"psm{i}", [128, 2]) for i in range(2)]
        n = 0
        for l in range(L):
            wv = W["w_ada"].ap[l].rearrange("(kc p) n -> p kc n", p=128)
            for blk in range(12):
                wb = wblk[blk % 2]
                K.dma(wb, V("w_ada", wv[:, :, blk * 512:(blk + 1) * 512]))
                for oc in range(4):
                    p = psm[n % 2]
                    n += 1
                    for kc in range(8):
                        K.mm(p, wb[:, kc, oc * 128:(oc + 1) * 128], scT[:, kc, :], start=(kc == 0), stop=(kc == 7))
                    idx = blk * 4 + oc
                    K.ts("dve", mod[:, l, idx, :], p, pvc(l, "b_ada", idx), None, ALU.add)
            for fc in range(8):
                K.ts("dve", sc1[:, l, fc, :], mod[:, l, 8 + fc, :], 1.0, pvc(l, "g1", fc), ALU.add, ALU.mult)
                K.ts("dve", sc2[:, l, fc, :], mod[:, l, 32 + fc, :], 1.0, pvc(l, "g2", fc), ALU.add, ALU.mult)
            K.ts("dve", omka[:, l, :], pvc(l, "ka", 0, 2), -1.0, 1.0, ALU.mult, ALU.add)
        zt = K.sb("zt", [128, 8, 1], BF16)
        K.memset("dve", zt, 0.0)
        for dst in (hT_d, h2T_d):
            for c in (0, TC + 1, TC + 2, NCOL - 1):
                K.dma(dst[:, :, c:c + 1].k((dst.key, "z", c)), zt, allow_slow_non_contiguous=True)
        K.end_phase()

        def norm_to_fm(xt, scale_v, shift_v, seg, dst, col0, tiles):
            sq, ss, rstd, xn, pT, hsb = tiles
            K.act(sq, xt, AF.Square, accum=ss)
            K.act(rstd, ss, AF.Sqrt, bias=epsc, scale=1.0 / D)
            K.recip(rstd, rstd)
            K.ts("dve", xn, xt, rstd, None, ALU.mult)
            for fc in range(8):
                K.tr(pT[:, fc, :], xn[:, fc * 128:(fc + 1) * 128], cm[:, IDENT, :])
            for fc in range(8):
                K.act(hsb[:, fc, :], pT[:, fc, :], AF.Identity, bias=shift_v(fc), scale=scale_v(fc))
            K.dma(dst[:, :, col0:col0 + 128].k((dst.key, col0)), hsb)

        def x_src(l, st):
            if l == 0:
                if st < 2:
                    return ctx_in[st * 128:(st + 1) * 128, :].k(("ctx", st))
                return x_in[(st - 2) * 128:(st - 1) * 128, :].k(("x", st))
            return xres[st * 128:(st + 1) * 128, :].k(("xres", st))


        class Rot:
            def __init__(self, items):
                self.items = items
                self.i = 0

            def next(self):
                t = self.items[self.i % len(self.items)]
                self.i += 1
                return t

        def order_for(dr):
            if dr == 0:
                return list(range(NCH))
            return [1, 0] + list(range(NCH - 1, 1, -1))

        def p2_weights(l):
            sbx = lambda n_, sh_, dt_: K.sb(n_, sh_, dt_, x=True)
            win = sbx("win", [128, 8, 3584], BF16)
            wv = W["w_in"].ap[l].rearrange("(kc p) n -> p kc n", p=128)
            for kc in range(8):
                K.dma(win[:, kc, 0:3072], V("w_in", wv[:, kc, :]), q="pool")
            for pi, part in enumerate((4, 5)):
                src = W["w_in"].ap[l][:, part * 256:(part + 1) * 256].rearrange(
                    "(kc p) (h two d) -> p kc h two d", p=128, h=4, two=2)
                dstv = win.ap[:, :, 3072 + pi * 256:3072 + (pi + 1) * 256].rearrange(
                    "p kc (h two d) -> p kc h two d", h=4, two=2)
                for two in range(2):
                    for kc in range(8):
                        K.dma(V(win.key, dstv[:, kc, :, two, :]), V("w_in", src[:, kc, :, 1 - two, :]), q="pool")
            wa1 = sbx("wa1", [128, 8, 64], BF16)
            K.memset("pool", wa1, 0.0)
            wa2 = sbx("wa2", [64, 256], BF16)
            w1 = sbx("w1", [128, 8, 128], BF16)
            w2 = sbx("w2", [128, 256], BF16)
            a1 = sbx("a1", [128, 8, 128], BF16)
            a2 = sbx("a2", [128, 256], BF16)
            for dr in range(2):
                K.dma(wa1[:, :, dr * 32:dr * 32 + 16],
                      V("gla_wa1", W["gla_wa1"].ap[l, dr].rearrange("(kc p) n -> p kc n", p=128)), q="pool")
                K.dma(wa2[dr * 32:dr * 32 + 16, :], V("gla_wa2", W["gla_wa2"].ap[l, dr]), q="pool")
                K.dma(w1[:, :, dr * 64:(dr + 1) * 64],
                      V("rwkv_w1", W["rwkv_w1"].ap[l, dr].rearrange("(kc p) n -> p kc n", p=128)), q="pool")
                K.dma(w2[dr * 64:(dr + 1) * 64, :], V("rwkv_w2", W["rwkv_w2"].ap[l, dr]), q="pool")
                K.dma(a1[:, :, dr * 64:(dr + 1) * 64],
                      V("rwkv_a1", W["rwkv_a1"].ap[l, dr].rearrange("(kc p) n -> p kc n", p=128)), q="pool")
                K.dma(a2[dr * 64:(dr + 1) * 64, :], V("rwkv_a2", W["rwkv_a2"].ap[l, dr]), q="pool")
            g1w = sbx("g1w", [128, 8, 160], BF16)
            K.dma(g1w, V("rwkv_g1", W["rwkv_g1"].ap[l].rearrange("(kc p) n -> p kc n", p=128)), q="pool")
            g2a = sbx("g2a", [128, 256], BF16)
            g2b = sbx("g2b", [32, 256], BF16)
            K.dma(g2a, V("rwkv_g2", W["rwkv_g2"].ap[l][0:128, :]), q="pool")
            K.dma(g2b, V("rwkv_g2", W["rwkv_g2"].ap[l][128:160, :]), q="pool")
            tmv = sbx("tmv", [128, 4, 256], F32)
            K.dma(tmv, tmv_in[l])
            return (win, wa1, wa2, w1, w2, a1, a2, g1w, g2a, g2b, tmv)

        def phase2(l, wts):
            K.begin_phase()
            win, wa1, wa2, w1, w2, a1, a2, g1w, g2a, g2b, tmv = wts
            hts = Rot([K.sb(f"ht{i}", [128, 8, 514], BF16) for i in range(1)])
            xx = K.sb("xx", [128, 8, 512], F32)
            xmix = K.sb("xmix", [128, 8, 512], BF16)
            pm = Rot([K.ps(f"pm{i}", [128, 512]) for i in range(5)])
            phs = K.ps("phs", [128, 16])
            phr = Rot([phs[:, 2 * i:2 * i + 2] for i in range(8)])
            psT = Rot([K.ps(f"psT{i}", [128, 512], BF16) for i in range(2)])
            qA = K.sb("qA", [128, 4, 512], BF16)
            kA = K.sb("kA", [128, 4, 512], BF16)
            gA = K.sb("gA", [128, 4, 512], BF16)
            fA = K.sb("fA", [128, 2, 512], BF16)
            ropeT = K.sb("ropeT", [128, 2, 512], F32)
            tr1 = Rot([K.sb(f"tr1_{i}", [128, 512], F32) for i in range(2)])
            tr2 = Rot([K.sb(f"tr2_{i}", [128, 512], F32) for i in range(1)])
            tm512 = Rot([K.sb(f"tm512_{i}", [128, 512], BF16) for i in range(2)])
            tm256 = Rot([K.sb(f"tm256_{i}", [128, 256], BF16) for i in range(2)])
            z1b = K.sb("z1b", [64, 512], BF16)
            zbs = Rot([K.sb(f"zb{i}", [128, 256], F32) for i in range(2)])
            la_s = Rot([K.sb(f"las{i}", [128, 256], F32) for i in range(2)])
            raws = Rot([K.sb(f"raw{i}", [128, 514], F32) for i in range(2)])
            rkvF = [K.sb(f"rkvF{i}", [128, 2, 512], F32) for i in range(3)]
            kkn = K.sb("kkn", [128, 2, 512], F32)
            sqb = Rot([K.sb(f"sqb{i}", [128, 512], BF16) for i in range(2)])
            rsr = Rot([K.sb(f"rsr{i}", [128, 512], F32) for i in range(1)])
            twb = K.sb("twb", [128, 512], BF16)
            a1b = K.sb("a1b", [128, 512], BF16)
            sg1 = K.sb("sg1", [128, 512], BF16)
            sg2 = K.sb("sg2", [32, 512], BF16)
            aF = [K.sb(f"aF{i}", [128, 2, 512], F32) for i in range(2)]
            rB = K.sb("rB", [128, 2, 512], BF16)
            aB = K.sb("aB", [128, 2, 512], BF16)
            gB = K.sb("gB", [128, 2, 512], BF16)
            vB = K.sb("vB", [128, 2, 512], BF16)
            bonB = K.sb("bonB", [128, 2, 512], BF16)
            bB = [K.sb(f"bB{i}", [128, 2, 512], BF16) for i in range(2)]
            kdB = [K.sb(f"kdB{i}", [128, 2, 512], BF16) for i in range(2)]
            tmpk = Rot([K.sb(f"tmpk{i}", [128, 512], F32) for i in range(1)])
            kdf = Rot([K.sb(f"kdf{i}", [128, 512], F32) for i in range(2)])
            rkd = Rot([K.sb(f"rkd{i}", [128, 512], BF16) for i in range(2)])
            idb = cmb[:, IDENT, :]
            obd = cmb[:, ONESBD, :]

            def fm_mm(ps, wt, c0_, rhs, Wd):
                for kc in range(8):
                    K.mm(ps[:, 0:Wd], wt[:, kc, c0_:c0_ + 128], rhs[:, kc, :], start=(kc == 0), stop=(kc == 7))

            for (seg, tok0, Wd) in cfg.tiles():
                c0 = cfg.col(tok0)
                nsub = Wd // 128
                ht = hts.next()
                K.dma(ht[:, :, 0:Wd + 2], hT_d[:, :, c0 - 1:c0 + Wd + 1])
                hc = ht[:, :, 1:Wd + 1]
                if seg == 1:
                    K.dma(ropeT[:, :, 0:Wd], rope_in[:, :, tok0 - TC:tok0 - TC + Wd])
                K.tt("dve", xx[:, :, 0:Wd], ht[:, :, 0:Wd], ht[:, :, 2:Wd + 2], ALU.add)
                K.stt("dve", xx[:, :, 0:Wd], xx[:, :, 0:Wd], 0.5, hc, ALU.mult, ALU.subtract)

                def mix(j):
                    for fc in range(8):
                        K.stt("dve", xmix[:, fc, 0:Wd], xx[:, fc, 0:Wd],
                              pvc(l, "mu", j * 8 + fc), ht[:, fc, 1:Wd + 1], ALU.mult, ALU.add)
                    return xmix
                for c in range(2):
                    p = pm.next()
                    fm_mm(p, win, 0 * 256 + c * 128, hc, Wd)
                    K.act(qA[:, c, 0:Wd], p[:, 0:Wd], AF.Copy, scale=0.125)
                    p = pm.next()
                    fm_mm(p, win, 1 * 256 + c * 128, hc, Wd)
                    K.copy("act", kA[:, c, 0:Wd], p[:, 0:Wd])
                    p = pm.next()
                    fm_mm(p, win, 3 * 256 + c * 128, hc, Wd)
                    K.act(gA[:, c, 0:Wd], p[:, 0:Wd], AF.Silu)
                    p = pm.next()
                    fm_mm(p, win, 7 * 256 + c * 128, hc, Wd)
                    K.act(gA[:, 2 + c, 0:Wd], p[:, 0:Wd], AF.Silu)
                    p = pm.next()
                    fm_mm(p, win, 11 * 256 + c * 128, hc, Wd)
                    K.copy("act", fA[:, c, 0:Wd], p[:, 0:Wd])
                    for (part, swp, dst, scl) in ((4, 3072, qA, 0.125), (5, 3328, kA, 1.0)):
                        p = pm.next()
                        fm_mm(p, win, part * 256 + c * 128, hc, Wd)
                        if seg == 0:
                            K.act(dst[:, 2 + c, 0:Wd], p[:, 0:Wd], AF.Copy, scale=scl)
                        else:
                            p2 = pm.next()
                            fm_mm(p2, win, swp + c * 128, hc, Wd)
                            ta, tb = tr1.next(), tr2.next()
                            K.stt("dve", ta[:, 0:Wd], p[:, 0:Wd], scl, ropeT[:, 0, 0:Wd], ALU.mult, ALU.mult)
                            K.stt("dve", tb[:, 0:Wd], p2[:, 0:Wd], scl, ropeT[:, 1, 0:Wd], ALU.mult, ALU.mult)
                            K.tt("dve", dst[:, 2 + c, 0:Wd], ta[:, 0:Wd], tb[:, 0:Wd], ALU.add)
                K.dma(A_qT[:, :, tok0:tok0 + Wd], qA[:, :, 0:Wd])
                K.dma(A_kT[:, :, tok0:tok0 + Wd], kA[:, :, 0:Wd])
                K.dma(A_gT[:, :, tok0:tok0 + Wd], gA[:, :, 0:Wd])
                K.dma(F_fT[:, :, tok0:tok0 + Wd], fA[:, :, 0:Wd])
                for j in range(nsub):
                    rows = slice(tok0 + j * 128, tok0 + (j + 1) * 128)
                    pt = psT.next()
                    for hp in range(4):
                        K.tr(pt[:, hp * 128:(hp + 1) * 128], kA[:, hp, j * 128:(j + 1) * 128], idb)
                    st_ = tm512.next()
                    K.copy("dve", st_, pt)
                    K.dma(A_k[rows, :].k(("A_k", tok0, j)), st_)
                    p = pm.next()
                    for hi, part in enumerate((2, 6)):
                        for kc in range(8):
                            K.mm(p[:, hi * 256:(hi + 1) * 256], ht[:, kc, 1 + j * 128:1 + (j + 1) * 128],
                                 win[:, kc, part * 256:(part + 1) * 256], start=(kc == 0), stop=(kc == 7))
                    st_ = tm512.next()
                    K.copy("act", st_, p)
                    K.dma(A_v[rows, :].k(("A_v", tok0, j)), st_)
                p = pm.next()
                for kc in range(8):
                    K.mm(p[0:64, 0:Wd], wa1[:, kc, :], ht[:, kc, 1:Wd + 1], start=(kc == 0), stop=(kc == 7))
                K.copy("act", z1b[:, 0:Wd], p[0:64, 0:Wd])
                for j in range(nsub):
                    rows = slice(tok0 + j * 128, tok0 + (j + 1) * 128)
                    for dr in range(2):
                        p = pm.next()
                        K.mm(p[:, 0:256], z1b[dr * 32:dr * 32 + 16, j * 128:(j + 1) * 128], wa2[dr * 32:dr * 32 + 16, :])
                        zb = zbs.next()
                        K.tt("dve", zb, p[:, 0:256], tmv[:, dr, :], ALU.add)
                        K.act(zb, zb, AF.Exp, scale=-1.0)
                        K.act(zb, zb, AF.Ln, bias=1.0)
                        ls = la_s.next()
                        K.act(ls, zb, AF.Copy, scale=-1.0 / 16.0)
                        K.dma(A_la[dr][rows, :].k((f"A_la{dr}", tok0, j)), ls)
                for X, part in ((0, 8), (1, 9), (2, 10)):
                    for c in range(2):
                        p = pm.next()
                        cc0 = part * 256 + c * 128
                        fm_mm(p, win, cc0, hc, Wd)
                        phh = phr.next()
                        for kc in range(8):
                            K.mm(phh, win[:, kc, cc0:cc0 + 128], ht[:, kc, 0:Wd + 2:Wd + 1], start=(kc == 0), stop=(kc == 7))
                        raw = raws.next()
                        K.copy("act", raw[:, 1:Wd + 1], p[:, 0:Wd])
                        K.copy("act", raw[:, 0:Wd + 2:Wd + 1], phh)
                        dst = rkvF[X][:, c, 0:Wd]
                        K.ts("dve", dst, raw[:, 0:Wd], pvc(l, "rconv", 0 * 6 + X * 2 + c), None, ALU.mult)
                        K.stt("dve", dst, raw[:, 1:Wd + 1], pvc(l, "rconv", 1 * 6 + X * 2 + c), dst, ALU.mult, ALU.add)
                        K.stt("dve", dst, raw[:, 2:Wd + 2], pvc(l, "rconv", 2 * 6 + X * 2 + c), dst, ALU.mult, ALU.add)
                rF, kF, vF = rkvF
                for c in range(2):
                    K.act(kkn[:, c, 0:Wd], kF[:, c, 0:Wd], AF.Identity, scale=pvc(l, "kkw", c), bias=0.0)
                    sq_ = sqb.next()
                    K.act(sq_[:, 0:Wd], kkn[:, c, 0:Wd], AF.Square)
                    p = pm.next()
                    K.mm(p[:, 0:Wd], obd, sq_[:, 0:Wd])
                    rs_ = rsr.next()
                    K.act(rs_[:, 0:Wd], p[:, 0:Wd], AF.Sqrt, bias=epsc)
                    K.recip(rs_[:, 0:Wd], rs_[:, 0:Wd])
                    K.tt("dve", kkn[:, c, 0:Wd], kkn[:, c, 0:Wd], rs_[:, 0:Wd], ALU.mult)
                    K.act(aB[:, c, 0:Wd], kkn[:, c, 0:Wd], AF.Copy, scale=-1.0)
                    K.copy("act", rB[:, c, 0:Wd], rF[:, c, 0:Wd])
                    K.copy("act", vB[:, c, 0:Wd], vF[:, c, 0:Wd])
                p = pm.next()
                xw = mix(0)
                fm_mm(p, w1, 0, xw[:, :, 0:Wd], Wd)
                K.act(twb[:, 0:Wd], p[:, 0:Wd], AF.Tanh)
                p = pm.next()
                xa = mix(1)
                fm_mm(p, a1, 0, xa[:, :, 0:Wd], Wd)
                K.copy("act", a1b[:, 0:Wd], p[:, 0:Wd])
                p = pm.next()
                xg = mix(2)
                fm_mm(p, g1w, 0, xg[:, :, 0:Wd], Wd)
                K.act(sg1[:, 0:Wd], p[:, 0:Wd], AF.Sigmoid)
                p = pm.next()
                for kc in range(8):
                    K.mm(p[0:32, 0:Wd], g1w[:, kc, 128:160], xg[:, kc, 0:Wd], start=(kc == 0), stop=(kc == 7))
                K.act(sg2[:, 0:Wd], p[0:32, 0:Wd], AF.Sigmoid)
                for c in range(2):
                    p = pm.next()
                    K.mm(p[:, 0:Wd], g2a[:, c * 128:(c + 1) * 128], sg1[:, 0:Wd], start=True, stop=False)
                    K.mm(p[:, 0:Wd], g2b[:, c * 128:(c + 1) * 128], sg2[:, 0:Wd], start=False, stop=True)
                    K.copy("act", gB[:, c, 0:Wd], p[:, 0:Wd])
                for j in range(nsub):
                    rows = slice(tok0 + j * 128, tok0 + (j + 1) * 128)
                    for dr in range(2):
                        p = pm.next()
                        K.mm(p[:, 0:256], twb[dr * 64:(dr + 1) * 64, j * 128:(j + 1) * 128], w2[dr * 64:(dr + 1) * 64, :])
                        zb = zbs.next()
                        K.tt("dve", zb, p[:, 0:256], tmv[:, 2 + dr, :], ALU.add)
                        K.act(zb, zb, AF.Sigmoid)
                        ls = la_s.next()
                        K.act(ls, zb, AF.Copy, scale=-math.exp(-0.5))
                        K.dma(B_lw[dr][rows, :].k((f"B_lw{dr}", tok0, j)), ls)
                for dr in range(2):
                    for c in range(2):
                        p = pm.next()
                        K.mm(p[:, 0:Wd], a2[dr * 64:(dr + 1) * 64, c * 128:(c + 1) * 128], a1b[dr * 64:(dr + 1) * 64, 0:Wd])
                        K.act(aF[dr][:, c, 0:Wd], p[:, 0:Wd], AF.Sigmoid, bias=pvc(l, "a0", dr * 2 + c))
                for c in range(2):
                    pb_ = pm.next()
                    for dr in range(2):
                        K.tt("dve", bB[dr][:, c, 0:Wd], kkn[:, c, 0:Wd], aF[dr][:, c, 0:Wd], ALU.mult)
                        tk, kd_, rk_ = tmpk.next(), kdf.next(), rkd.next()
                        K.act(tk[:, 0:Wd], aF[dr][:, c, 0:Wd], AF.Identity, scale=pvc(l, "ka", c), bias=omka[:, l, c:c + 1])
                        K.tt("dve", kd_[:, 0:Wd], kF[:, c, 0:Wd], tk[:, 0:Wd], ALU.mult)
                        K.copy("act", kdB[dr][:, c, 0:Wd], kd_[:, 0:Wd])
                        K.stt("dve", rk_[:, 0:Wd], rF[:, c, 0:Wd], pvc(l, "rk", c), kd_[:, 0:Wd], ALU.mult, ALU.mult)
                        K.mm(pb_[:, 0:Wd], obd, rk_[:, 0:Wd], start=(dr == 0), stop=(dr == 1))
                    K.tt("dve", bonB[:, c, 0:Wd], pb_[:, 0:Wd], vF[:, c, 0:Wd], ALU.mult)
                for j in range(nsub):
                    rows = slice(tok0 + j * 128, tok0 + (j + 1) * 128)
                    for ai, (src, dstd) in enumerate(((vB, B_v), (bB[0], B_b[0]), (bB[1], B_b[1]),
                                                      (kdB[0], B_kd[0]), (kdB[1], B_kd[1]))):
                        pt = psT.next()
                        for c in range(2):
                            K.tr(pt[:, c * 128:(c + 1) * 128], src[:, c, j * 128:(j + 1) * 128], idb)
                        st_ = tm256.next()
                        K.copy("act" if ai % 2 == 0 else "dve", st_, pt[:, 0:256])
                        K.dma(dstd[rows, :].k((dstd.key, tok0, j)), st_)
                for (src, dstd) in ((rB, B_rT), (aB, B_aT), (bB[0], B_bT[0]), (bB[1], B_bT[1]), (kdB[0], B_kdT[0]),
                                    (kdB[1], B_kdT[1]), (gB, B_gT), (bonB, B_bonT)):
                    K.dma(dstd[:, :, tok0:tok0 + Wd].k((dstd.key, tok0)), src[:, :, 0:Wd])
            K.end_phase()

        def headnorm_block(o_ap, centered, gn_ap, Wd, tiles, pmr, ones_f):
            oc_t, sq_t, rs_t = tiles
            if centered:
                p = pmr.next()
                K.mm(p[:, 0:Wd], ones_f, o_ap)
                K.stt("dve", oc_t[:, 0:Wd], p[:, 0:Wd], -1.0 / 64.0, o_ap, ALU.mult, ALU.add)
                src = oc_t[:, 0:Wd]
            else:
                src = o_ap
            K.act(sq_t[:, 0:Wd], src, AF.Square)
            p = pmr.next()
            K.mm(p[:, 0:Wd], ones_f, sq_t[:, 0:Wd])
            K.act(rs_t[:, 0:Wd], p[:, 0:Wd], AF.Sqrt, bias=epsc, scale=1.0 / 64.0)
            K.recip(rs_t[:, 0:Wd], rs_t[:, 0:Wd])
            K.stt("dve", oc_t[:, 0:Wd], src, gn_ap, rs_t[:, 0:Wd], ALU.mult, ALU.mult)
            return oc_t[:, 0:Wd]

        def phase3A(l):
            K.begin_phase()
            oacc = K.sb("oacc", [128, 4, NTOK], F32)
            Sst = K.sb("Sst", [128, 4, 64], F32)
            Sb = K.sb("Sb", [128, 4, 64], BF16)
            retla = K.sb("retla", [128, 2, 256], F32)
            K.dma(retla, retla_in)
            NBUF = 2
            qTs = Rot([K.sb(f"qT{i}", [128, 4, 128], BF16) for i in range(NBUF)])
            kTs = Rot([K.sb(f"kT{i}", [128, 4, 128], BF16) for i in range(NBUF)])
            kMs = Rot([K.sb(f"kM{i}", [128, 512], BF16) for i in range(NBUF)])
            vMs = Rot([K.sb(f"vM{i}", [128, 512], BF16) for i in range(NBUF)])
            las = Rot([K.sb(f"la{i}", [128, 256], F32) for i in range(NBUF)])
            Eb = K.sb("Eb", [128, 4, 128], F32)
            Enb = K.sb("Enb", [128, 4, 128], F32)
            Ec = K.sb("Ec", [128, 512], F32)
            qt = K.sb("qt", [128, 4, 128], BF16)
            kt = K.sb("kt", [128, 4, 128], BF16)
            kh = K.sb("kh", [128, 512], BF16)
            sc = K.sb("sc", [128, 8, 128], BF16)
            ps_b = K.ps("ps_b", [128, 4, 128])
            ps_c = K.ps("ps_c", [128, 512])
            ps_s = K.ps("ps_s", [128, 8, 128])
            ps_o = K.ps("ps_o", [128, 4, 128])
            ps_S = K.ps("ps_S", [128, 4, 64])
            for dr in range(2):
                K.memset("dve", Sst, 0.0)
                K.memset("pool", Sb, 0.0)
                TRI = cm[:, LE if dr == 0 else GE, :]
                AFT = cm[:, GT if dr == 0 else LT, :]
                last = 127 if dr == 0 else 0
                for ch in order_for(dr):
                    tok = ch * 128
                    qT, kT, kM, vM, la = qTs.next(), kTs.next(), kMs.next(), vMs.next(), las.next()
                    K.dma(qT, A_qT[:, :, tok:tok + 128])
                    K.dma(kT, A_kT[:, :, tok:tok + 128])
                    K.dma(kM, A_k[tok:tok + 128, :])
                    K.dma(vM, A_v[tok:tok + 128, :])
                    K.dma(la, A_la[dr][tok:tok + 128, :])
                    for hp in range(4):
                        lhs = la[:, hp * 128:(hp + 1) * 128] if hp < 2 else retla[:, dr, (hp - 2) * 128:(hp - 1) * 128]
                        K.mm(ps_b[:, hp, :], lhs, TRI)
                    K.act(Eb, ps_b, AF.Exp)
                    K.act(Enb, ps_b, AF.Exp, scale=-1.0)
                    K.mm(ps_c[:, 0:256], AFT, la)
                    K.mm(ps_c[:, 256:512], AFT, retla[:, dr, :])
                    K.act(Ec, ps_c, AF.Exp)
                    K.tt("dve", qt, qT, Eb, ALU.mult)
                    K.tt("pool", kt, kT, Enb, ALU.mult)
                    K.tt("pool", kh, kM, Ec, ALU.mult)
                    for h in range(8):
                        hp, pb = h // 2, (h % 2) * 64
                        K.mm(ps_s[:, (h % 2) * 4 + hp, :], kt[pb:pb + 64, hp, :], qt[pb:pb + 64, hp, :])
                    K.tt("dve", sc, ps_s, bc(TRI, [128, 8, 128], 1), ALU.mult)
                    for h in range(8):
                        hp, pb = h // 2, (h % 2) * 64
                        K.mm(ps_o[pb:pb + 64, hp, :], vM[:, h * 64:(h + 1) * 64], sc[:, (h % 2) * 4 + hp, :], start=True, stop=False)
                        K.mm(ps_o[pb:pb + 64, hp, :], Sb[pb:pb + 64, hp, :], qt[pb:pb + 64, hp, :], start=False, stop=True)
                    if dr == 0:
                        K.copy("act", oacc[:, :, tok:tok + 128], ps_o)
                    else:
                        K.tt("dve", oacc[:, :, tok:tok + 128], oacc[:, :, tok:tok + 128], ps_o, ALU.add)
                    for h in range(8):
                        hp, pb = h // 2, (h % 2) * 64
                        K.mm(ps_S[pb:pb + 64, hp, :], kh[:, h * 64:(h + 1) * 64], vM[:, h * 64:(h + 1) * 64])
                    dec = V(Eb.key, Eb.ap[:, :, last:last + 1].to_broadcast([128, 4, 64]))
                    K.tt("dve", Sst, Sst, dec, ALU.mult)
                    K.tt("dve", Sst, Sst, ps_S, ALU.add)
                    K.copy("act", Sb, Sst)
            ones_f = cm[:, ONESBD, :]
            pmr = Rot([K.ps("pe1", [128, 512]), K.ps("pe2", [128, 512])])
            octs = Rot([K.sb(f"oct{i}", [128, 512], F32) for i in range(2)])
            sqts = Rot([K.sb(f"sqt{i}", [128, 512], F32) for i in range(2)])
            rsts = Rot([K.sb(f"rst{i}", [128, 512], F32) for i in range(2)])
            gts = Rot([K.sb(f"gt{i}", [128, 4, 512], BF16) for i in range(2)])
            brs = Rot([K.sb(f"brs{i}", [128, 4, 512], BF16) for i in range(2)])
            for (seg, tok0, Wd) in cfg.tiles():
                gt_ = gts.next()
                br_ = brs.next()
                K.dma(gt_[:, :, 0:Wd], A_gT[:, :, tok0:tok0 + Wd])
                for hp in range(4):
                    gn = pvc(l, "gla_gn", hp) if hp < 2 else pvc(l, "ret_gn", hp - 2)
                    res_ = headnorm_block(oacc[:, hp, tok0:tok0 + Wd], hp >= 2, gn, Wd,
                                          (octs.next(), sqts.next(), rsts.next()), pmr, ones_f)
                    K.tt("pool", br_[:, hp, 0:Wd], res_, gt_[:, hp, 0:Wd], ALU.mult)
                K.dma(BR[:, 0:4, tok0:tok0 + Wd].k(("BR", "a", tok0)), br_[:, :, 0:Wd])
            K.end_phase()

        def phase3B(l, prefetch=None):
            K.begin_phase()
            pre_out = prefetch() if prefetch is not None else None
            yacc = K.sb("yacc", [128, 2, NTOK], F32)
            MSK = [K.sb(f"MSK{i}", [128, 4, 128], F32) for i in range(2)]
            for dr in range(2):
                st_m, in_m = (BLT, BLE) if dr == 0 else (BGT, BGE)
                for kind, mk in enumerate((st_m, in_m, st_m, in_m)):
                    K.copy("pool", MSK[dr][:, kind, :], cm[:, mk, :])
            B0 = K.ps("B0", [128, 512])
            B1 = K.ps("B1", [128, 512])
            PA = [K.ps(f"PA{i}", [128, 512]) for i in range(4)]
            PN = K.ps("PN", [128, 4, 128])
            B7 = K.ps("B7", [128, 512])
            T4 = [V(PA[i].key, PA[i].ap.rearrange("p (a b) -> p a b", a=4)) for i in range(4)]
            Yps = V(B1.key, B1.ap[:, 256:512].rearrange("p (a b) -> p a b", a=2))
            Hps = V(PA[0].key, PA[0].ap[:, 0:128].rearrange("p (a b) -> p a b", a=2))
            identb = bc(cmb[:, IDENT, :], [128, 4, 128], 1)

            def mkdir_(dr):
                H = K.sb("H", [128, 2, 64], F32)
                Hb = K.sb("Hb", [128, 2, 64], BF16)
                NB = 2
                rTs = Rot([K.sb(f"rT{i}", [128, 2, 128], BF16) for i in range(NB)])
                aTs = Rot([K.sb(f"aT{i}", [128, 2, 128], BF16) for i in range(NB)])
                bTs = Rot([K.sb(f"bT{i}", [128, 2, 128], BF16) for i in range(NB)])
                kdTs = Rot([K.sb(f"kdT{i}", [128, 2, 128], BF16) for i in range(NB)])
                vMs = Rot([K.sb(f"vM{i}", [128, 256], BF16) for i in range(NB)])
                bMs = Rot([K.sb(f"bM{i}", [128, 256], BF16) for i in range(NB)])
                kdMs = Rot([K.sb(f"kdM{i}", [128, 256], BF16) for i in range(NB)])
                lws = Rot([K.sb(f"lw{i}", [128, 256], F32) for i in range(NB)])
                Ei = K.sb("Ei", [128, 2, 128], F32)
                Eni = K.sb("Eni", [128, 2, 128], F32)
                Ee = K.sb("Ee", [128, 2, 128], F32)
                Ea = K.sb("Ea", [128, 256], F32)
                arz = [K.sb(f"arz{i}", [128, 2, 2, 128], BF16) for i in range(2)]
                bt = K.sb("bt", [128, 2, 128], BF16)
                kt = K.sb("kt", [128, 2, 128], BF16)
                bh = K.sb("bh", [128, 256], BF16)
                khz = [K.sb(f"khz{i}", [128, 256], BF16) for i in range(2)]
                Rbz = [K.sb(f"Rbz{i}", [128, 256], BF16) for i in range(2)]
                Ubz = [K.sb(f"Ubz{i}", [128, 256], BF16) for i in range(2)]
                AM = K.sb("AM", [128, 4, 4, 128], BF16)
                Nb = K.sb("Nb", [128, 4, 128], BF16)
                Q = K.sb("Q", [128, 4, 128], BF16)
                Qt = K.sb("Qt", [128, 4, 128], BF16)
                M2b = [K.sb(f"M2b{i}", [128, 4, 128], BF16) for i in range(2)]
                N2b = [K.sb(f"N2b{i}", [128, 4, 128], BF16) for i in range(2)]
                for t_ in arz + khz + Rbz + Ubz:
                    K.memset("pool", t_, 0.0)
                Ei2 = V(Ei.key, Ei.ap.rearrange("p a b -> p (a b)"))
                Eni2 = V(Eni.key, Eni.ap.rearrange("p a b -> p (a b)"))
                Ee2 = V(Ee.key, Ee.ap.rearrange("p a b -> p (a b)"))
                K.memset("dve", H, 0.0)
                K.memset("pool", Hb, 0.0)
                return dict(locals())

            DT = [mkdir_(0), mkdir_(1)]
            visited = set()

            def proc(dr, ch):
                g = DT[dr]
                H, Hb, rTs, aTs, bTs, kdTs, vMs, bMs, kdMs, lws, Ei, Eni, Ee, Ea, arz, bt, kt, bh, khz, Rbz, Ubz, AM, Nb, Q, Qt, M2b, N2b, Ei2, Eni2, Ee2 = (g[k_] for k_ in ['H', 'Hb', 'rTs', 'aTs', 'bTs', 'kdTs', 'vMs', 'bMs', 'kdMs', 'lws', 'Ei', 'Eni', 'Ee', 'Ea', 'arz', 'bt', 'kt', 'bh', 'khz', 'Rbz', 'Ubz', 'AM', 'Nb', 'Q', 'Qt', 'M2b', 'N2b', 'Ei2', 'Eni2', 'Ee2'])
                INC, EXC, AFT = (BLE, BLT, BGT) if dr == 0 else (BGE, BGT, BLT)
                tok = ch * 128
                rT, aT, bT, kdT = rTs.next(), aTs.next(), bTs.next(), kdTs.next()
                vM, bM, kdM, lw = vMs.next(), bMs.next(), kdMs.next(), lws.next()
                K.dma(rT, B_rT[:, :, tok:tok + 128])
                K.dma(aT, B_aT[:, :, tok:tok + 128])
                K.dma(bT, B_bT[dr][:, :, tok:tok + 128])
                K.dma(kdT, B_kdT[dr][:, :, tok:tok + 128])
                K.dma(vM, B_v[tok:tok + 128, :])
                K.dma(bM, B_b[dr][tok:tok + 128, :])
                K.dma(kdM, B_kd[dr][tok:tok + 128, :])
                K.dma(lw, B_lw[dr][tok:tok + 128, :])
                for hp in range(2):
                    K.mm(B0[:, hp * 128:(hp + 1) * 128], lw[:, hp * 128:(hp + 1) * 128], cm[:, INC, :])
                    K.mm(B0[:, 256 + hp * 128:256 + (hp + 1) * 128], lw[:, hp * 128:(hp + 1) * 128], cm[:, EXC, :])
                K.mm(B1[:, 0:256], cm[:, AFT, :], lw)
                K.act(Ei2, B0[:, 0:256], AF.Exp)
                K.act(Eni2, B0[:, 0:256], AF.Exp, scale=-1.0)
                K.act(Ee2, B0[:, 256:512], AF.Exp)
                K.act(Ea, B1[:, 0:256], AF.Exp)
                for par in range(2):
                    pr = slice(par * 64, par * 64 + 64)
                    K.tt("dve" if par == 0 else "pool", arz[par][pr, :, 0, :], aT[pr, :, :], Ee[pr, :, :], ALU.mult)
                    K.tt("pool" if par == 0 else "dve", arz[par][pr, :, 1, :], rT[pr, :, :], Ei[pr, :, :], ALU.mult)
                K.tt("dve", bt, bT, Eni, ALU.mult)
                K.tt("pool", kt, kdT, Eni, ALU.mult)
                K.tt("pool", bh, bM, Ea, ALU.mult)
                for half in range(2):
                    hs = slice(64 * half, 64 * half + 64)
                    K.tt("dve" if half == 0 else "pool", khz[half][hs, :], kdM[hs, :], Ea[hs, :], ALU.mult)
                for h in range(4):
                    hp, z = h // 2, arz[h % 2]
                    K.mm(PA[h][:, 0:256], bt[:, hp, :], z[:, hp, :, :])
                    K.mm(PA[h][:, 256:512], kt[:, hp, :], z[:, hp, :, :])
                    K.mm(PN[:, h, :], z[:, hp, 0, :], bt[:, hp, :])
                for h in range(4):
                    K.tt("dve", V(AM.key, AM.ap[:, h, :, :].rearrange("p a b -> p (a b)")), PA[h],
                         V(MSK[dr].key, MSK[dr].ap.rearrange("p a b -> p (a b)")), ALU.mult)
                K.tt("dve", Nb, PN, bc(cm[:, AFT, :], [128, 4, 128], 1), ALU.mult)
                Mk = AM[:, :, 0, :]
                Nk = Nb
                K.tt("pool", Q, Mk, identb, ALU.add)
                K.tt("pool", Qt, Nk, identb, ALU.add)
                for lev in range(5):
                    for h in range(4):
                        K.mm(T4[0][:, h, :], Nk[:, h, :], Mk[:, h, :])
                        K.mm(T4[1][:, h, :], Mk[:, h, :], Nk[:, h, :])
                    Mn, Nn = M2b[lev % 2], N2b[lev % 2]
                    K.copy("act", Mn, T4[0])
                    K.copy("dve", Nn, T4[1])
                    for h in range(4):
                        K.mm(T4[2][:, h, :], Qt[:, h, :], Mn[:, h, :])
                        if lev < 4:
                            K.mm(T4[3][:, h, :], Q[:, h, :], Nn[:, h, :])
                    K.tt("dve", Q, Q, T4[2], ALU.add)
                    if lev < 4:
                        K.tt("dve", Qt, Qt, T4[3], ALU.add)
                    Mk, Nk = Mn, Nn
                for half in ((0, 1) if dr == 0 else (1, 0)):
                    hs = slice(64 * half, 64 * half + 64)
                    Rb, Ub = Rbz[half], Ubz[half]
                    for h in range(4):
                        hp, z = h // 2, arz[h % 2]
                        K.mm(B7[hs, h * 64:(h + 1) * 64], z[:, hp, 0, hs], Hb[:, hp, :], start=True, stop=False)
                        K.mm(B7[hs, h * 64:(h + 1) * 64], AM[:, h, 2, hs], vM[:, h * 64:(h + 1) * 64], start=False, stop=True)
                    K.copy("act", Rb[hs, :], B7[hs, 0:256])
                    for h in range(4):
                        K.mm(B7[hs, 256 + h * 64:256 + (h + 1) * 64], Q[:, h, hs], Rb[:, h * 64:(h + 1) * 64])
                    K.copy("dve", Ub[hs, :], B7[hs, 256:512])
                    for h in range(4):
                        hp, pb, z = h // 2, (h % 2) * 64, arz[h % 2]
                        o_ = Yps[pb:pb + 64, hp, hs]
                        K.mm(o_, Hb[:, hp, :], z[:, hp, 1, hs], start=True, stop=False)
                        K.mm(o_, Ub[:, h * 64:(h + 1) * 64], AM[:, h, 1, hs], start=False, stop=False)
                        K.mm(o_, vM[:, h * 64:(h + 1) * 64], AM[:, h, 3, hs], start=False, stop=True)
                    for h in range(4):
                        hp, pb = h // 2, (h % 2) * 64
                        o_ = Hps[pb:pb + 64, hp, :]
                        K.mm(o_, bh[:, h * 64:(h + 1) * 64], Ub[:, h * 64:(h + 1) * 64], start=True, stop=False)
                        K.mm(o_, khz[half][:, h * 64:(h + 1) * 64], vM[:, h * 64:(h + 1) * 64], start=False, stop=True)
                    endc = 64 * half + (63 if dr == 0 else 0)
                    dec = V(Ei.key, Ei.ap[:, :, endc:endc + 1].to_broadcast([128, 2, 64]))
                    K.tt("dve", H, H, dec, ALU.mult)
                    K.tt("dve", H, H, Hps, ALU.add)
                    K.copy("act", Hb, H)
                if ch not in visited:
                    visited.add(ch)
                    K.copy("act", yacc[:, :, tok:tok + 128], Yps)
                else:
                    K.tt("dve", yacc[:, :, tok:tok + 128], yacc[:, :, tok:tok + 128], Yps, ALU.add)

            orders = [order_for(0), order_for(1)]
            for k_ in range(NCH):
                for dr in range(2):
                    proc(dr, orders[dr][k_])
            ones_f = cm[:, ONESBD, :]
            pmr = Rot([B0, B7])
            octs = Rot([K.sb(f"oct{i}", [128, 512], F32) for i in range(2)])
            sqts = Rot([K.sb(f"sqt{i}", [128, 512], F32) for i in range(2)])
            rsts = Rot([K.sb(f"rst{i}", [128, 512], F32) for i in range(2)])
            gts = Rot([K.sb(f"gt{i}", [128, 2, 512], BF16) for i in range(2)])
            bns = Rot([K.sb(f"bn{i}", [128, 2, 512], BF16) for i in range(2)])
            brs = Rot([K.sb(f"brs{i}", [128, 2, 512], BF16) for i in range(2)])
            for (seg, tok0, Wd) in cfg.tiles():
                gt_, bn_, br_ = gts.next(), bns.next(), brs.next()
                K.dma(gt_[:, :, 0:Wd], B_gT[:, :, tok0:tok0 + Wd])
                K.dma(bn_[:, :, 0:Wd], B_bonT[:, :, tok0:tok0 + Wd])
                for c in range(2):
                    res_ = headnorm_block(yacc[:, c, tok0:tok0 + Wd], True, pvc(l, "rwkv_gn", c), Wd,
                                          (octs.next(), sqts.next(), rsts.next()), pmr, ones_f)
                    K.tt("pool", res_, res_, bn_[:, c, 0:Wd], ALU.add)
                    K.tt("pool", br_[:, c, 0:Wd], res_, gt_[:, c, 0:Wd], ALU.mult)
                K.dma(BR[:, 4:6, tok0:tok0 + Wd].k(("BR", "b", tok0)), br_[:, :, 0:Wd])
            K.end_phase()
            return pre_out

        def phase3F(l):
            K.begin_phase()
            fT = K.sb("fT", [128, 2, NTOK], BF16)
            K.dma(fT, F_fT)
            d64 = K.sb("d64", [128, 2, 128], BF16)
            K.dma(d64, dft64_in, q="pool")
            G = K.sb("G", [128, NCH, 2, 256], BF16)
            pg = Rot([K.ps(f"pg{i}", [128, 512]) for i in range(2)])
            po = Rot([K.ps(f"po{i}", [128, 512]) for i in range(3)])
            for ch in range(NCH):
                p = pg.next()
                for k in range(2):
                    for c in range(2):
                        K.mm(p[:, k * 256 + c * 128:k * 256 + (c + 1) * 128], fT[:, c, ch * 128:(ch + 1) * 128], d64[:, k, :])
                K.copy("act", G[:, ch, 0, :], p[:, 0:256])
                K.ts("dve", G[:, ch, 1, :], p[:, 256:512], -1.0, None, ALU.mult)
            maxch = max(2, NCH - 2)
            tabr = Rot([K.sb(f"tab{i}", [128, maxch, 512], BF16) for i in range(3)])
            obr = Rot([K.sb(f"ob{i}", [128, 512], BF16) for i in range(2)])
            for (sname, ch0, nch, T, tab_in) in (("c", 0, 2, TC, dftC_in), ("l", 2, NCH - 2, TL, dftL_in)):
                for n0 in range(0, T, 512):
                    nw = min(512, T - n0)
                    tabs = []
                    for k in range(2):
                        tb = tabr.next()
                        src = tab_in.ap[k].rearrange("(ch p) n -> p ch n", p=128)
                        for j0 in range(0, nch, 8):
                            j1 = min(nch, j0 + 8)
                            K.dma(tb[:, j0:j1, 0:nw], V("dft", src[:, j0:j1, n0:n0 + nw]))
                        tabs.append(tb)
                    for c in range(2):
                        p = po.next()
                        first = True
                        for k in range(2):
                            for j in range(nch):
                                K.mm(p[:, 0:nw], G[:, ch0 + j, k, c * 128:(c + 1) * 128], tabs[k][:, j, 0:nw],
                                     start=first, stop=(k == 1 and j == nch - 1))
                                first = False
                        ob = obr.next()
                        K.copy("act", ob[:, 0:nw], p[:, 0:nw])
                        K.dma(BR[:, 6 + c, ch0 * 128 + n0:ch0 * 128 + n0 + nw].k(("BR", "f", c, n0, sname)), ob[:, 0:nw])
            K.end_phase()

        def tm_bcast(dst, l, base, pz, onesf, diags):
            for j in range(2):
                for fc in range(8):
                    d_ = diags.next()
                    K.ts("dve", d_, cm[:, IDENT, :], mod[:, l, base + fc, j:j + 1], None, ALU.mult)
                    p = pz.next()
                    K.mm(p[:, 0:128], onesf, d_)
                    K.copy("act", dst[:, j, fc * 128:(fc + 1) * 128], p[:, 0:128])

        def p4_weights(l):
            wg = K.sb("wg", [128, 8, 4096], BF16, x=True)
            for i in range(4):
                src = W["w_gate"].ap[l, i].rearrange("(kc p) n -> p kc n", p=128)
                for kc in range(8):
                    K.dma(wg[:, kc, i * 1024:(i + 1) * 1024], V("w_gate", src[:, kc, :]), q="pool")
            return wg

        def phase4(l, last, wg):
            K.begin_phase()
            wbr = K.sb("wbr", [128, 8, 1024], BF16)
            for i in range(4):
                K.dma(wbr[:, i * 2:(i + 1) * 2, :], V("w_br", W["w_br"].ap[l, i].rearrange("(c p) n -> p c n", p=128)), q="pool")
            wo = K.sb("wo", [128, 8, 1024], BF16)
            srco = W["w_out"].ap[l].rearrange("(kc p) n -> p kc n", p=128)
            for kc in range(8):
                K.dma(wo[:, kc, :], V("w_out", srco[:, kc, :]), q="pool")
            onesf = K.sb("onesf", [128, 128], F32)
            K.memset("dve", onesf, 1.0)
            m2tm = K.sb("m2tm", [128, 2, 1024], F32)
            pz = Rot([K.ps(f"pz{i}", [128, 512]) for i in range(6)])
            diags = Rot([K.sb(f"diag{i}", [128, 128], F32) for i in range(2)])
            tm_bcast(m2tm, l, 16, pz, onesf, diags)
            ht = K.sb("ht4", [128, 8, 512], BF16)
            brt = K.sb("brt", [128, 8, 512], BF16)
            zT = K.sb("zT", [128, 8, 512], BF16)
            gsr = Rot([K.sb(f"gs{i}", [128, 512], F32) for i in range(3)])
            zaccs = Rot([K.sb(f"zacc{i}", [128, 512], F32) for i in range(2)])
            xts = Rot([K.sb(f"xt4_{i}", [128, D], F32) for i in range(2)])
            tmps = Rot([K.sb(f"tmp4_{i}", [128, 512], F32) for i in range(2)])
            sq = K.sb("sq4", [128, D], F32)
            ss = K.sb("ss4", [128, 1], F32)
            rs = K.sb("rs4", [128, 1], F32)
            xn = K.sb("xn4", [128, D], F32)
            pT = K.ps("pT4", [128, 8, 128])
            hsb = K.sb("hsb4", [128, 8, 128], BF16)
            for (seg, tok0, Wd) in cfg.tiles():
                if last and seg == 0:
                    continue
                jm = 1 - seg
                c0 = cfg.col(tok0)
                K.dma(ht[:, :, 0:Wd], hT_d[:, :, c0:c0 + Wd])
                K.dma(brt[:, :, 0:Wd], BR[:, :, tok0:tok0 + Wd])
                for oc in range(8):
                    zacc = zaccs.next()
                    for i in range(4):
                        pgt = pz.next()
                        for kc in range(8):
                            K.mm(pgt[:, 0:Wd], wg[:, kc, i * 1024 + oc * 128:i * 1024 + (oc + 1) * 128], ht[:, kc, 0:Wd],
                                 start=(kc == 0), stop=(kc == 7))
                        gs = gsr.next()
                        K.act(gs[:, 0:Wd], pgt[:, 0:Wd], AF.Sigmoid, bias=pvc(l, "b_gate", i * 8 + oc))
                        pb_ = pz.next()
                        for c in range(2):
                            K.mm(pb_[:, 0:Wd], wbr[:, i * 2 + c, oc * 128:(oc + 1) * 128], brt[:, i * 2 + c, 0:Wd],
                                 start=(c == 0), stop=(c == 1))
                        if i == 0:
                            K.tt("dve", zacc[:, 0:Wd], gs[:, 0:Wd], pb_[:, 0:Wd], ALU.mult)
                        else:
                            K.tt("dve", gs[:, 0:Wd], gs[:, 0:Wd], pb_[:, 0:Wd], ALU.mult)
                            K.tt("dve", zacc[:, 0:Wd], zacc[:, 0:Wd], gs[:, 0:Wd], ALU.add)
                    K.copy("act", zT[:, oc, 0:Wd], zacc[:, 0:Wd])
                for j in range(Wd // 128):
                    st = tok0 // 128 + j
                    xt = xts.next()
                    K.dma(xt, x_src(l, st))
                    for half in range(2):
                        p = pz.next()
                        for oc in range(8):
                            K.mm(p, zT[:, oc, j * 128:(j + 1) * 128], wo[:, oc, half * 512:(half + 1) * 512],
                                 start=(oc == 0), stop=(oc == 7))
                        tmp = tmps.next()
                        K.tt("dve", tmp, p, m2tm[:, jm, half * 512:(half + 1) * 512], ALU.mult)
                        K.tt("dve", xt[:, half * 512:(half + 1) * 512], xt[:, half * 512:(half + 1) * 512], tmp, ALU.add)
                    K.dma(xres[st * 128:(st + 1) * 128, :].k(("xres", st)), xt)
                    norm_to_fm(xt, lambda fc: sc2[:, l, fc, jm:jm + 1], lambda fc: mod[:, l, 24 + fc, jm:jm + 1], seg,
                               h2T_d, cfg.col(st * 128), (sq, ss, rs, xn, pT, hsb))
            K.end_phase()

        def phase5(l, last):
            K.begin_phase()
            srcu = W["ffn_up"].ap[l].rearrange("(kc p) n -> p kc n", p=128)
            NC_ = FF // 128
            blks = [(0, 6), (6, 12), (12, 17), (17, 22)]
            wupA, wupU = [], []
            for bi, (ca, cb) in enumerate(blks):
                for hh, lst in ((0, wupA), (1, wupU)):
                    t_ = K.sb(f"wup{hh}_{bi}", [128, 8, (cb - ca) * 128], BF16)
                    for kc in range(8):
                        K.dma(t_[:, kc, :], V("ffn_up", srcu[:, kc, hh * FF + ca * 128:hh * FF + cb * 128]), q="pool")
                    lst.append(t_)

            def wsel(lst, cc):
                for bi, (ca, cb) in enumerate(blks):
                    if ca <= cc < cb:
                        return lst[bi], (cc - ca) * 128
            wdn = K.sb("wdn", [128, NC_, D], BF16)
            srcd = W["ffn_down"].ap[l].rearrange("(c p) n -> p c n", p=128)
            for c_ in range(0, NC_, 4):
                c1 = min(NC_, c_ + 4)
                K.dma(wdn[:, c_:c1, :], V("ffn_down", srcd[:, c_:c1, :]), q="pool")
            onesf = K.sb("onesf5", [128, 128], F32)
            K.memset("dve", onesf, 1.0)
            m5tm = K.sb("m5tm", [128, 2, 1024], F32)
            pz = Rot([K.ps(f"pz5_{i}", [128, 512]) for i in range(6)])
            phs = K.ps("phs5", [128, 16])
            phr = Rot([phs[:, 2 * i:2 * i + 2] for i in range(8)])
            diags = Rot([K.sb(f"diag5_{i}", [128, 128], F32) for i in range(2)])
            tm_bcast(m5tm, l, 40, pz, onesf, diags)
            h2t = K.sb("h2t", [128, 8, 514], BF16)
            gt = K.sb("gt5", [128, NC_, 512], BF16)
            cvs = Rot([K.sb(f"cv5_{i}", [128, 512], F32) for i in range(3)])
            xts = Rot([K.sb(f"xt5_{i}", [128, D], F32) for i in range(2)])
            tmps = Rot([K.sb(f"tmp5_{i}", [128, 512], F32) for i in range(2)])
            for (seg, tok0, Wd) in cfg.tiles():
                if last and seg == 0:
                    continue
                jm = 1 - seg
                c0 = cfg.col(tok0)
                K.dma(h2t[:, :, 0:Wd + 2], h2T_d[:, :, c0 - 1:c0 + Wd + 1])
                for cc in range(NC_):
                    pa = pz.next()
                    wa_, oa_ = wsel(wupA, cc)
                    wu_, ou_ = wsel(wupU, cc)
                    for kc in range(8):
                        K.mm(pa[:, 0:Wd], wa_[:, kc, oa_:oa_ + 128], h2t[:, kc, 1:Wd + 1], start=(kc == 0), stop=(kc == 7))
                    phh = phr.next()
                    for kc in range(8):
                        K.mm(phh, wa_[:, kc, oa_:oa_ + 128], h2t[:, kc, 0:Wd + 2:Wd + 1], start=(kc == 0), stop=(kc == 7))
                    pu = pz.next()
                    for kc in range(8):
                        K.mm(pu[:, 0:Wd], wu_[:, kc, ou_:ou_ + 128], h2t[:, kc, 1:Wd + 1],
                             start=(kc == 0), stop=(kc == 7))
                    cv = cvs.next()
                    w0_, w1_, w2_ = (pvc(l, "fconv", j_ * NC_ + cc) for j_ in range(3))
                    K.act(cv[:, 0:Wd], pa[:, 0:Wd], AF.Identity, scale=w1_, bias=0.0)
                    K.stt("dve", cv[:, 1:Wd], pa[:, 0:Wd - 1], w0_, cv[:, 1:Wd], ALU.mult, ALU.add)
                    K.stt("dve", cv[:, 0:Wd - 1], pa[:, 1:Wd], w2_, cv[:, 0:Wd - 1], ALU.mult, ALU.add)
                    K.stt("dve", cv[:, 0:1], phh[:, 0:1], w0_, cv[:, 0:1], ALU.mult, ALU.add)
                    K.stt("dve", cv[:, Wd - 1:Wd], phh[:, 1:2], w2_, cv[:, Wd - 1:Wd], ALU.mult, ALU.add)
                    K.act(cv[:, 0:Wd], cv[:, 0:Wd], AF.Silu, bias=pvc(l, "fconvb", cc))
                    K.tt("dve", gt[:, cc, 0:Wd], cv[:, 0:Wd], pu[:, 0:Wd], ALU.mult)
                for j in range(Wd // 128):
                    st = tok0 // 128 + j
                    xt = xts.next()
                    K.dma(xt, xres[st * 128:(st + 1) * 128, :].k(("xres", st)))
                    for half in range(2):
                        p = pz.next()
                        for cc in range(NC_):
                            K.mm(p, gt[:, cc, j * 128:(j + 1) * 128], wdn[:, cc, half * 512:(half + 1) * 512],
                                 start=(cc == 0), stop=(cc == NC_ - 1))
                        tmp = tmps.next()
                        K.tt("dve", tmp, p, m5tm[:, jm, half * 512:(half + 1) * 512], ALU.mult)
                        K.tt("dve", xt[:, half * 512:(half + 1) * 512], xt[:, half * 512:(half + 1) * 512], tmp, ALU.add)
                    K.dma(xres[st * 128:(st + 1) * 128, :].k(("xres", st)), xt)
            K.end_phase()

        def phase6():
            K.begin_phase()
            gf = K.sb("gf", [128, D], F32)
            K.dma(gf, gfin_in)
            xts = Rot([K.sb(f"xt6_{i}", [128, D], F32) for i in range(2)])
            sqs = Rot([K.sb(f"sq6_{i}", [128, D], F32) for i in range(2)])
            sss = Rot([K.sb(f"ss6_{i}", [128, 1], F32) for i in range(2)])
            ots = Rot([K.sb(f"ot6_{i}", [128, D], F32) for i in range(2)])
            for st in range(2, NCH):
                xt, sq, ss, ot = xts.next(), sqs.next(), sss.next(), ots.next()
                K.dma(xt, xres[st * 128:(st + 1) * 128, :].k(("xres", st)))
                K.act(sq, xt, AF.Square, accum=ss)
                K.act(ss, ss, AF.Sqrt, bias=epsc, scale=1.0 / D)
                K.recip(ss, ss)
                K.stt("dve", ot, xt, ss, gf, ALU.mult, ALU.mult)
                K.dma(out_d[(st - 2) * 128:(st - 1) * 128, :].k(("out", st)), ot, is_output=True)
            K.end_phase()

        for l in range(L):
            last = l == L - 1
            K.xbegin()
            K.begin_phase()
            wts2 = p2_weights(l)
            NB = 3
            xts = [K.sb(f"xt{i}", [128, D], F32) for i in range(NB)]
            sqs = [K.sb(f"sq{i}", [128, D], F32) for i in range(NB)]
            sss = [K.sb(f"ss{i}", [128, 1], F32) for i in range(NB)]
            rss = [K.sb(f"rs{i}", [128, 1], F32) for i in range(NB)]
            xns = [K.sb(f"xn{i}", [128, D], F32) for i in range(NB)]
            pTs = [K.ps(f"pT{i}", [128, 8, 128]) for i in range(NB)]
            hss = [K.sb(f"hs{i}", [128, 8, 128], BF16) for i in range(NB)]
            for st in range(NCH):
                i = st % NB
                seg = 0 if st < 2 else 1
                j = 1 - seg
                K.dma(xts[i], x_src(l, st))
                norm_to_fm(xts[i], lambda fc: sc1[:, l, fc, j:j + 1], lambda fc: mod[:, l, fc, j:j + 1], seg,
                           hT_d, cfg.col(st * 128), (sqs[i], sss[i], rss[i], xns[i], pTs[i], hss[i]))
            K.end_phase()
            if cfg.stop_after == ("p1", l):
                break
            phase2(l, wts2)
            K.xend()
            phase3F(l)
            phase3A(l)
            K.xbegin()
            wg_ = phase3B(l, prefetch=lambda: p4_weights(l))
            phase4(l, last, wg_)
            K.xend()
            if cfg.stop_after == ("p4", l):
                break
            phase5(l, last)
            if cfg.stop_after == ("p5", l):
                break
        if cfg.stop_after is None:
            phase6()

    return nc


def _fm_cols(v):
    v = np.asarray(v, np.float32)
    return np.ascontiguousarray(v.reshape(-1, 128).T)


def host_consts(TL):
    i = np.arange(128)
    jj, ii = np.meshgrid(i, i, indexing="ij")
    same = (jj // 64) == (ii // 64)
    mats = [np.eye(128), same.astype(np.float64), jj <= ii, jj < ii, jj >= ii, jj > ii,
            same & (jj <= ii), same & (jj < ii), same & (jj >= ii), same & (jj > ii)]
    cmask = np.stack([m.astype(np.float32) for m in mats], axis=1)
    t = np.arange(TL)
    row = (t // 64).astype(np.float32)
    colv = (t % 64).astype(np.float32)
    nfreq = 16
    inv = (np.float32(10000.0) ** (-np.arange(nfreq, dtype=np.float32) / nfreq)).astype(np.float32)
    ang = np.concatenate([row[:, None] * inv, colv[:, None] * inv], axis=-1).astype(np.float32)
    cosT, sinT = np.cos(ang), np.sin(ang)
    rope = np.zeros((128, 2, TL), np.float32)
    for p in range(128):
        d = p % 64
        rope[p, 0] = cosT[:, d % 32]
        rope[p, 1] = -sinT[:, d % 32] if d < 32 else sinT[:, d % 32]
    retla = np.zeros((128, 2, 256), np.float32)
    for dr in range(2):
        expo = -5.0 - np.arange(4, dtype=np.float32)
        if dr == 1:
            expo = expo[::-1]
        lg = np.log1p(-np.exp2(expo)).astype(np.float32)
        retla[:, dr, :] = np.repeat(lg, 64)[None, :]

    def dft(T):
        k = np.arange(T, dtype=np.int64)
        m = (k[:, None] * k[None, :]) % T
        a = 2.0 * np.pi * m.astype(np.float64) / T
        s = 1.0 / np.sqrt(T)
        return np.stack([np.cos(a) * s, np.sin(a) * s]).astype(ml_dtypes.bfloat16)

    d64 = np.zeros((128, 2, 128), np.float32)
    k = np.arange(64)
    a = 2.0 * np.pi * ((k[:, None] * k[None, :]) % 64) / 64.0
    for hb in range(2):
        d64[hb * 64:(hb + 1) * 64, 0, hb * 64:(hb + 1) * 64] = np.cos(a) / 8.0
        d64[hb * 64:(hb + 1) * 64, 1, hb * 64:(hb + 1) * 64] = np.sin(a) / 8.0
    return dict(cmask=cmask, rope=rope, retla=retla, dftL=dft(TL), dftC=dft(TC), dft64=d64)


def host_layout(inp, b, consts):
    m = dict(consts)
    m["x"] = np.ascontiguousarray(inp["x"][b], dtype=np.float32)
    m["ctx"] = np.ascontiguousarray(inp["ctx"][b], dtype=np.float32)
    cT = np.zeros((128, 8, 2), np.float32)
    cT[:, :, 0] = _fm_cols(inp["c"][b])
    cT[:, :, 1] = _fm_cols(inp["c_ctx"])
    m["cT"] = cT
    pv = np.zeros((NLAYER, 128, NPV), np.float32)
    tmv = np.zeros((NLAYER, 128, 4, 256), np.float32)
    for l in range(NLAYER):
        def put(name, arr):
            o, c = PV[name]
            pv[l, :, o:o + c] = arr
        put("b_ada", _fm_cols(inp["b_ada"][l]))
        put("g1", _fm_cols(inp["g_norm1"][l]))
        put("g2", _fm_cols(inp["g_norm2"][l]))
        put("mu", np.concatenate([_fm_cols(inp["rwkv_mu"][l][j]) for j in range(3)], axis=1))
        put("gla_gn", _fm_cols(inp["gla_gn"][l]))
        put("ret_gn", _fm_cols(inp["ret_gn"][l]))
        put("rconv", np.concatenate([_fm_cols(inp["rwkv_conv"][l][j]) for j in range(3)], axis=1))
        put("a0", np.concatenate([_fm_cols(inp["rwkv_a0"][l][j]) for j in range(2)], axis=1))
        put("kkw", _fm_cols(inp["rwkv_kk"][l]))
        put("ka", _fm_cols(inp["rwkv_ka"][l]))
        put("rk", _fm_cols(inp["rwkv_rk"][l]))
        put("rwkv_gn", _fm_cols(inp["rwkv_gn"][l]))
        put("b_gate", np.concatenate([_fm_cols(inp["b_gate"][l][j]) for j in range(4)], axis=1))
        put("fconv", np.concatenate([_fm_cols(inp["ffn_conv"][l][j]) for j in range(3)], axis=1))
        put("fconvb", _fm_cols(inp["ffn_conv_b"][l]))
        for j in range(2):
            tmv[l, :, j, :] = np.asarray(inp["gla_ba"][l][j], np.float32)[None, :]
            tmv[l, :, 2 + j, :] = np.asarray(inp["rwkv_w0"][l][j], np.float32)[None, :]
    m["pv"] = pv
    m["tmv"] = tmv
    m["gfin"] = np.ascontiguousarray(np.broadcast_to(np.asarray(inp["g_final"], np.float32)[None, :], (128, D)))
    for n in WEIGHT_NAMES:
        m[n] = np.ascontiguousarray(inp[n], dtype=np.float32)
    return m


_CACHE = {}


def kernel(**inputs):
    TL = inputs["x"].shape[1]
    B = inputs["x"].shape[0]
    key = ("nc", TL)
    if key not in _CACHE:
        _CACHE[key] = (build(Cfg(TL=TL)), host_consts(TL))
    nc, consts = _CACHE[key]
    in_maps = [host_layout(inputs, b, consts) for b in range(B)]
    res = run_bass_kernel_spmd(nc, in_maps, core_ids=list(range(B)))
    return np.stack([np.asarray(r["out"], np.float32) for r in res.results], axis=0)
```

```python
import math
from contextlib import ExitStack
import numpy as np
import ml_dtypes
import concourse.bass as bass
import concourse.mybir as mybir
from concourse.bass_utils import run_bass_kernel_spmd

F32 = mybir.dt.float32
BF16 = mybir.dt.bfloat16
ALU = mybir.AluOpType
AF = mybir.ActivationFunctionType

D = 1024
TC = 256
NLAYER = 2
FF = 2816
EPS = 1e-6
COMPUTE = ("pe", "act", "dve", "pool")
QUEUES = ("sp", "pool")
EPOCH = 30000
RING = 6


class V:
    __slots__ = ("key", "ap")

    def __init__(self, key, ap):
        self.key = key
        self.ap = ap

    def __getitem__(self, idx):
        return V(self.key, self.ap[idx])

    def k(self, key):
        return V(key, self.ap)


class Sched:
    def __init__(self, nc, stack):
        self.nc = nc
        self.stack = stack
        self.q = {e: [] for e in ("pe", "act", "dve", "pool", "sp")}
        self.cnt = {e: 0 for e in COMPUTE}
        self.rank = {e: 0 for e in COMPUTE}
        self.targets = {e: set() for e in COMPUTE}
        self.esems = {e: [] for e in COMPUTE}
        self.seen_c = {e: {} for e in self.q}
        self.seen_d = {e: {} for e in self.q}
        self.bufs = {}
        self.ring = {}
        self.ringcnt = {}
        for qn in QUEUES:
            self.ring[qn] = [self._sem(f"dq_{qn}_{i}") for i in range(RING)]
            self.ringcnt[qn] = 0
        self.ninst = 0

    def _sem(self, name):
        return self.stack.enter_context(self.nc.semaphore(name))

    def _engine_sem(self, e, idx):
        while len(self.esems[e]) <= idx:
            self.esems[e].append(self._sem(f"e_{e}_{len(self.esems[e])}"))
        return self.esems[e][idx]

    def _need(self, eng, ref):
        if ref[0] == "c":
            _, src, idx = ref
            if self.seen_c[eng].get(src, 0) >= idx:
                return
            self.seen_c[eng][src] = idx
            self.targets[src].add(idx)
            self.q[eng].append(("waitc", src, idx))
        else:
            _, sem, val, _qn = ref
            key = id(sem)
            if self.seen_d[eng].get(key, 0) >= val:
                return
            self.seen_d[eng][key] = val
            self.q[eng].append(("waitd", sem, val))

    def _deps(self, reads, writes):
        deps = []
        for b in reads:
            st = self.bufs.get(b)
            if st and st[0] is not None:
                deps.append((st[0], True))
        for b in writes:
            st = self.bufs.get(b)
            if st:
                if st[0] is not None:
                    deps.append((st[0], False))
                for r in st[1]:
                    deps.append((r, False))
        return deps

    def _update(self, ref, reads, writes):
        for b in reads:
            st = self.bufs.setdefault(b, [None, []])
            if ref[0] == "c":
                st[1] = [r for r in st[1] if not (r[0] == "c" and r[1] == ref[1])]
            st[1].append(ref)
        for b in writes:
            self.bufs[b] = [ref, []]

    def op(self, eng, fn, reads=(), writes=()):
        for (d, raw) in self._deps(reads, writes):
            if d[0] == "c" and d[1] == eng and not raw:
                continue
            self._need(eng, d)
        self.cnt[eng] += 1
        idx = self.cnt[eng]
        self.q[eng].append(("op", fn, idx))
        self._update(("c", eng, idx), reads, writes)
        self.ninst += 1

    def dma(self, qn, out, in_, is_output=False, **kw):
        reads, writes = [in_.key], [out.key]
        for (d, raw) in self._deps(reads, writes):
            self._need(qn, d)
        i = self.ringcnt[qn]
        self.ringcnt[qn] += 1
        sem = self.ring[qn][i % RING]
        prev = (i // RING) * 16
        if prev > 0:
            self._need(qn, ("d", sem, prev, qn))
        val = prev + 16
        oa, ia = out.ap, in_.ap

        def fn(e, oa=oa, ia=ia, kw=kw):
            return e.dma_start(out=oa, in_=ia, **kw)

        self.q[qn].append(("dma", fn, sem))
        self._update(("d", sem, val, qn), reads, writes)
        self.ninst += 1

    def barrier(self):
        refs = []
        for e in COMPUTE:
            if self.cnt[e] > 0:
                refs.append(("c", e, self.cnt[e]))
        for qn in QUEUES:
            n = self.ringcnt[qn]
            for j in range(min(RING, n)):
                cntj = (n - 1 - j) // RING + 1
                refs.append(("d", self.ring[qn][j], cntj * 16, qn))
        for e in self.q:
            for ref in refs:
                self._need(e, ref)
        self.bufs = {}

    def emit_block(self):
        nc = self.nc
        q = self.q
        self.q = {e: [] for e in q}
        semval = {}
        for e in COMPUTE:
            for idx in sorted(self.targets[e]):
                self.rank[e] += 1
                r = self.rank[e]
                semval[(e, idx)] = (self._engine_sem(e, (r - 1) // EPOCH), (r - 1) % EPOCH + 1)
            self.targets[e] = set()
        with nc.Block() as block:
            def mk(name):
                items = q[name]

                def body(e):
                    for item in items:
                        kind = item[0]
                        if kind == "waitc":
                            sem, val = semval[(item[1], item[2])]
                            e.wait_ge(sem, val)
                        elif kind == "waitd":
                            e.wait_ge(item[1], item[2])
                        elif kind == "dma":
                            item[1](e).then_inc(item[2], 16)
                        else:
                            inst = item[1](e)
                            sv = semval.get((name, item[2]))
                            if sv is not None:
                                inst.then_inc(sv[0], 1)
                return body
            block.tensor(mk("pe"))
            block.scalar(mk("act"))
            block.vector(mk("dve"))
            block.gpsimd(mk("pool"))
            block.sync(mk("sp"))


class KB:
    def __init__(self, nc, S):
        self.nc = nc
        self.S = S
        self.pstack = None
        self.uid = 0

    def xbegin(self):
        self.xstack = ExitStack()
        self.xstack.__enter__()

    def xend(self):
        self.xstack.__exit__(None, None, None)
        self.xstack = None

    def sb(self, name, shape, dt, persistent=False, x=False):
        st = self.S.stack if persistent else (self.xstack if x else self.pstack)
        self.uid += 1
        t = st.enter_context(self.nc.sbuf_tensor(f"{name}_{self.uid}", list(shape), dt))
        return V(f"{name}_{self.uid}", t[:])

    def ps(self, name, shape, dt=F32):
        self.uid += 1
        esz = 4 if dt == F32 else 2
        n = 1
        for d_ in shape[1:]:
            n *= d_
        nbanks = (n * esz + 2047) // 2048
        t = self.pstack.enter_context(self.nc.psum_tensor(f"{name}_{self.uid}", [128, 512 * nbanks], F32))
        ap = t[:]
        if dt != F32:
            ap = ap.bitcast(dt)
        ap = ap[0:shape[0], 0:n]
        if len(shape) == 3:
            ap = ap.rearrange("p (a b) -> p a b", a=shape[1])
        return V(f"{name}_{self.uid}", ap)

    def begin_phase(self):
        self.pstack = ExitStack()
        self.pstack.__enter__()

    def end_phase(self):
        self.S.barrier()
        self.S.emit_block()
        self.pstack.__exit__(None, None, None)
        self.pstack = None

    def _rw(self, outs, ins):
        return [v.key for v in ins if isinstance(v, V)], [v.key for v in outs]

    def mm(self, out, lhsT, rhs, start=True, stop=True, tp=None):
        r, w = self._rw([out], [lhsT, rhs])
        oa, la, ra = out.ap, lhsT.ap, rhs.ap
        if tp is None:
            self.S.op("pe", lambda e: e.matmul(oa, la, ra, start=start, stop=stop), r, w)
        else:
            self.S.op("pe", lambda e: e.matmul(oa, la, ra, start=start, stop=stop, tile_position=tp), r, w)

    def tr(self, out, in_, ident):
        r, w = self._rw([out], [in_, ident])
        oa, ia, da = out.ap, in_.ap, ident.ap
        self.S.op("pe", lambda e: e.transpose(oa, ia, da), r, w)

    def act(self, out, in_, func, bias=None, scale=None, accum=None, eng="act"):
        ins = [in_]
        kw = {}
        if bias is not None:
            if isinstance(bias, V):
                ins.append(bias)
                kw["bias"] = bias.ap
            else:
                kw["bias"] = float(bias)
        if scale is not None:
            if isinstance(scale, V):
                ins.append(scale)
                kw["scale"] = scale.ap
            else:
                kw["scale"] = float(scale)
        outs = [out]
        if accum is not None:
            outs.append(accum)
            kw["accum_out"] = accum.ap
        r, w = self._rw(outs, ins)
        oa, ia = out.ap, in_.ap
        self.S.op("act", lambda e: e.activation(oa, ia, func, **kw), r, w)

    def tt(self, eng, out, in0, in1, op):
        r, w = self._rw([out], [in0, in1])
        oa, a0, a1 = out.ap, in0.ap, in1.ap
        self.S.op(eng, lambda e: e.tensor_tensor(oa, a0, a1, op), r, w)

    def ts(self, eng, out, in0, s1, s2=None, op0=ALU.mult, op1=None):
        ins = [in0]
        a1 = s1
        a2 = s2
        if isinstance(s1, V):
            ins.append(s1)
            a1 = s1.ap
        if isinstance(s2, V):
            ins.append(s2)
            a2 = s2.ap
        r, w = self._rw([out], ins)
        oa, ia = out.ap, in0.ap
        if op1 is None:
            self.S.op(eng, lambda e: e.tensor_scalar(oa, ia, a1, None, op0), r, w)
        else:
            self.S.op(eng, lambda e: e.tensor_scalar(oa, ia, a1, a2, op0, op1), r, w)

    def stt(self, eng, out, in0, scalar, in1, op0, op1):
        ins = [in0, in1]
        sa = scalar
        if isinstance(scalar, V):
            ins.append(scalar)
            sa = scalar.ap
        r, w = self._rw([out], ins)
        oa, a0, a1 = out.ap, in0.ap, in1.ap
        eng = "dve"
        self.S.op(eng, lambda e: e.scalar_tensor_tensor(oa, a0, sa, a1, op0, op1), r, w)

    def recip(self, out, in_):
        r, w = self._rw([out], [in_])
        oa, ia = out.ap, in_.ap
        self.S.op("dve", lambda e: e.reciprocal(oa, ia), r, w)

    def copy(self, eng, out, in_):
        if eng == "act":
            return self.act(out, in_, AF.Copy)
        r, w = self._rw([out], [in_])
        oa, ia = out.ap, in_.ap
        self.S.op(eng, lambda e: e.tensor_copy(oa, ia), r, w)

    def memset(self, eng, out, val):
        r, w = self._rw([out], [])
        oa = out.ap
        self.S.op(eng, lambda e: e.memset(oa, val), r, w)

    def dma(self, out, in_, q="sp", **kw):
        self.S.dma(q, out, in_, **kw)


def bc(v, shape, axis):
    return V(v.key, v.ap.unsqueeze(axis).to_broadcast(list(shape)))


class Cfg:
    def __init__(self, TL=4096, nlayer=NLAYER, stop_after=None, debug=()):
        self.TL = TL
        self.NTOK = TC + TL
        self.NCH = self.NTOK // 128
        self.NCOL = self.NTOK + 4
        self.nlayer = nlayer
        self.stop_after = stop_after
        self.debug = tuple(debug)

    def col(self, tok):
        return tok + 1 if tok < TC else tok + 3

    def tiles(self):
        out = [(0, 0, TC)]
        for s in range(0, self.TL, 512):
            out.append((1, TC + s, min(512, self.TL - s)))
        return out


WEIGHT_NAMES = ["w_ada", "w_in", "gla_wa1", "gla_wa2", "rwkv_w1", "rwkv_w2", "rwkv_a1", "rwkv_a2",
                "rwkv_g1", "rwkv_g2", "w_gate", "w_br", "w_out", "ffn_up", "ffn_down"]
WEIGHT_SHAPES = {
    "w_ada": [NLAYER, D, 6 * D], "w_in": [NLAYER, D, 3072], "gla_wa1": [NLAYER, 2, D, 16],
    "gla_wa2": [NLAYER, 2, 16, 256], "rwkv_w1": [NLAYER, 2, D, 64], "rwkv_w2": [NLAYER, 2, 64, 256],
    "rwkv_a1": [NLAYER, 2, D, 64], "rwkv_a2": [NLAYER, 2, 64, 256], "rwkv_g1": [NLAYER, D, 160],
    "rwkv_g2": [NLAYER, 160, 256], "w_gate": [NLAYER, 4, D, D], "w_br": [NLAYER, 4, 256, D],
    "w_out": [NLAYER, D, D], "ffn_up": [NLAYER, D, 2 * FF], "ffn_down": [NLAYER, FF, D],
}

PV = {}
_o = 0
for _n, _c in [("b_ada", 48), ("g1", 8), ("g2", 8), ("mu", 24), ("gla_gn", 2), ("ret_gn", 2), ("rconv", 18),
               ("a0", 4), ("kkw", 2), ("ka", 2), ("rk", 2), ("rwkv_gn", 2), ("b_gate", 32), ("fconv", 66),
               ("fconvb", 22)]:
    PV[_n] = (_o, _c)
    _o += _c
NPV = _o


def build(cfg):
    nc = bass.Bass("TRN2", target_bir_lowering=False)
    TL, NTOK, NCH, NCOL = cfg.TL, cfg.NTOK, cfg.NCH, cfg.NCOL
    L = cfg.nlayer

    def din(name, shape, dt=F32):
        return V(name, nc.dram_tensor(name, list(shape), dt, kind="ExternalInput").ap())

    def dscr(name, shape, dt=F32):
        kind = "ExternalOutput" if name in cfg.debug else "Internal"
        return V(name, nc.dram_tensor(name, list(shape), dt, kind=kind).ap())

    x_in = din("x", [TL, D])
    ctx_in = din("ctx", [TC, D])
    cT_in = din("cT", [128, 8, 2])
    pv_in = din("pv", [NLAYER, 128, NPV])
    tmv_in = din("tmv", [NLAYER, 128, 4, 256])
    gfin_in = din("gfin", [128, D])
    cm_in = din("cmask", [128, 10, 128])
    rope_in = din("rope", [128, 2, TL])
    retla_in = din("retla", [128, 2, 256])
    dftL_in = din("dftL", [2, TL, TL], BF16)
    dftC_in = din("dftC", [2, TC, TC], BF16)
    dft64_in = din("dft64", [128, 2, 128])
    W = {n: din(n, WEIGHT_SHAPES[n]) for n in WEIGHT_NAMES}
    out_d = V("out", nc.dram_tensor("out", [TL, D], F32, kind="ExternalOutput").ap())

    xres = dscr("xres", [NTOK, D])
    hT_d = dscr("hT", [128, 8, NCOL], BF16)
    h2T_d = dscr("h2T", [128, 8, NCOL], BF16)
    A_qT = dscr("A_qT", [128, 4, NTOK], BF16)
    A_kT = dscr("A_kT", [128, 4, NTOK], BF16)
    A_gT = dscr("A_gT", [128, 4, NTOK], BF16)
    A_k = dscr("A_k", [NTOK, 512], BF16)
    A_v = dscr("A_v", [NTOK, 512], BF16)
    A_la = [dscr(f"A_la{d}", [NTOK, 256]) for d in range(2)]
    B_rT = dscr("B_rT", [128, 2, NTOK], BF16)
    B_aT = dscr("B_aT", [128, 2, NTOK], BF16)
    B_bT = [dscr(f"B_bT{d}", [128, 2, NTOK], BF16) for d in range(2)]
    B_kdT = [dscr(f"B_kdT{d}", [128, 2, NTOK], BF16) for d in range(2)]
    B_gT = dscr("B_gT", [128, 2, NTOK], BF16)
    B_bonT = dscr("B_bonT", [128, 2, NTOK], BF16)
    B_v = dscr("B_v", [NTOK, 256], BF16)
    B_b = [dscr(f"B_b{d}", [NTOK, 256], BF16) for d in range(2)]
    B_kd = [dscr(f"B_kd{d}", [NTOK, 256], BF16) for d in range(2)]
    B_lw = [dscr(f"B_lw{d}", [NTOK, 256]) for d in range(2)]
    F_fT = dscr("F_fT", [128, 2, NTOK], BF16)
    BR = dscr("BR", [128, 8, NTOK], BF16)

    with ExitStack() as top:
        S = Sched(nc, top)
        K = KB(nc, S)

        cm = K.sb("cm", [128, 10, 128], F32, True)
        cmb = K.sb("cmb", [128, 10, 128], BF16, True)
        pv = K.sb("pv", [128, NLAYER, NPV], F32, True)
        mod = K.sb("mod", [128, NLAYER, 48, 2], F32, True)
        sc1 = K.sb("sc1", [128, NLAYER, 8, 2], F32, True)
        sc2 = K.sb("sc2", [128, NLAYER, 8, 2], F32, True)
        omka = K.sb("omka", [128, NLAYER, 2], F32, True)
        epsc = K.sb("epsc", [128, 1], F32, True)
        IDENT, ONESBD, LE, LT, GE, GT, BLE, BLT, BGE, BGT = range(10)

        def pvc(l, name, i=0, n=1):
            o, c = PV[name]
            return pv[:, l, o + i:o + i + n]

        K.begin_phase()
        K.dma(cm, cm_in)
        K.dma(cmb, cm_in, q="pool")
        for l in range(L):
            K.dma(pv[:, l, :], pv_in[l])
        K.memset("dve", epsc, EPS)
        cT = K.sb("cT", [128, 8, 2], F32)
        scT = K.sb("scT", [128, 8, 2], F32)
        K.dma(cT, cT_in)
        K.act(scT, cT, AF.Silu)
        wblk = [K.sb(f"wblk{i}", [128, 8, 512], F32) for i in range(2)]
        psm = [K.ps(f"psm{i}", [128, 2]) for i in range(2)]
        n = 0
        for l in range(L):
            wv = W["w_ada"].ap[l].rearrange("(kc p) n -> p kc n", p=128)
            for blk in range(12):
                wb = wblk[blk % 2]
                K.dma(wb, V("w_ada", wv[:, :, blk * 512:(blk + 1) * 512]))
                for oc in range(4):
                    p = psm[n % 2]
                    n += 1
                    for kc in range(8):
                        K.mm(p, wb[:, kc, oc * 128:(oc + 1) * 128], scT[:, kc, :], start=(kc == 0), stop=(kc == 7))
                    idx = blk * 4 + oc
                    K.ts("dve", mod[:, l, idx, :], p, pvc(l, "b_ada", idx), None, ALU.add)
            for fc in range(8):
                K.ts("dve", sc1[:, l, fc, :], mod[:, l, 8 + fc, :], 1.0, pvc(l, "g1", fc), ALU.add, ALU.mult)
                K.ts("dve", sc2[:, l, fc, :], mod[:, l, 32 + fc, :], 1.0, pvc(l, "g2", fc), ALU.add, ALU.mult)
            K.ts("dve", omka[:, l, :], pvc(l, "ka", 0, 2), -1.0, 1.0, ALU.mult, ALU.add)
        zt = K.sb("zt", [128, 8, 1], BF16)
        K.memset("dve", zt, 0.0)
        for dst in (hT_d, h2T_d):
            for c in (0, TC + 1, TC + 2, NCOL - 1):
                K.dma(dst[:, :, c:c + 1].k((dst.key, "z", c)), zt, allow_slow_non_contiguous=True)
        K.end_phase()

        def norm_to_fm(xt, scale_v, shift_v, seg, dst, col0, tiles):
            sq, ss, rstd, xn, pT, hsb = tiles
            K.act(sq, xt, AF.Square, accum=ss)
            K.act(rstd, ss, AF.Sqrt, bias=epsc, scale=1.0 / D)
            K.recip(rstd, rstd)
            K.ts("dve", xn, xt, rstd, None, ALU.mult)
            for fc in range(8):
                K.tr(pT[:, fc, :], xn[:, fc * 128:(fc + 1) * 128], cm[:, IDENT, :])
            for fc in range(8):
                K.act(hsb[:, fc, :], pT[:, fc, :], AF.Identity, bias=shift_v(fc), scale=scale_v(fc))
            K.dma(dst[:, :, col0:col0 + 128].k((dst.key, col0)), hsb)

        def x_src(l, st):
            if l == 0:
                if st < 2:
                    return ctx_in[st * 128:(st + 1) * 128, :].k(("ctx", st))
                return x_in[(st - 2) * 128:(st - 1) * 128, :].k(("x", st))
            return xres[st * 128:(st + 1) * 128, :].k(("xres", st))


        class Rot:
            def __init__(self, items):
                self.items = items
                self.i = 0

            def next(self):
                t = self.items[self.i % len(self.items)]
                self.i += 1
                return t

        def order_for(dr):
            if dr == 0:
                return list(range(NCH))
            return [1, 0] + list(range(NCH - 1, 1, -1))

        def p2_weights(l):
            sbx = lambda n_, sh_, dt_: K.sb(n_, sh_, dt_, x=True)
            win = sbx("win", [128, 8, 3584], BF16)
            wv = W["w_in"].ap[l].rearrange("(kc p) n -> p kc n", p=128)
            for kc in range(8):
                K.dma(win[:, kc, 0:3072], V("w_in", wv[:, kc, :]), q="pool")
            for pi, part in enumerate((4, 5)):
                src = W["w_in"].ap[l][:, part * 256:(part + 1) * 256].rearrange(
                    "(kc p) (h two d) -> p kc h two d", p=128, h=4, two=2)
                dstv = win.ap[:, :, 3072 + pi * 256:3072 + (pi + 1) * 256].rearrange(
                    "p kc (h two d) -> p kc h two d", h=4, two=2)
                for two in range(2):
                    for kc in range(8):
                        K.dma(V(win.key, dstv[:, kc, :, two, :]), V("w_in", src[:, kc, :, 1 - two, :]), q="pool")
            wa1 = sbx("wa1", [128, 8, 64], BF16)
            K.memset("pool", wa1, 0.0)
            wa2 = sbx("wa2", [64, 256], BF16)
            w1 = sbx("w1", [128, 8, 128], BF16)
            w2 = sbx("w2", [128, 256], BF16)
            a1 = sbx("a1", [128, 8, 128], BF16)
            a2 = sbx("a2", [128, 256], BF16)
            for dr in range(2):
                K.dma(wa1[:, :, dr * 32:dr * 32 + 16],
                      V("gla_wa1", W["gla_wa1"].ap[l, dr].rearrange("(kc p) n -> p kc n", p=128)), q="pool")
                K.dma(wa2[dr * 32:dr * 32 + 16, :], V("gla_wa2", W["gla_wa2"].ap[l, dr]), q="pool")
                K.dma(w1[:, :, dr * 64:(dr + 1) * 64],
                      V("rwkv_w1", W["rwkv_w1"].ap[l, dr].rearrange("(kc p) n -> p kc n", p=128)), q="pool")
                K.dma(w2[dr * 64:(dr + 1) * 64, :], V("rwkv_w2", W["rwkv_w2"].ap[l, dr]), q="pool")
                K.dma(a1[:, :, dr * 64:(dr + 1) * 64],
                      V("rwkv_a1", W["rwkv_a1"].ap[l, dr].rearrange("(kc p) n -> p kc n", p=128)), q="pool")
                K.dma(a2[dr * 64:(dr + 1) * 64, :], V("rwkv_a2", W["rwkv_a2"].ap[l, dr]), q="pool")
            g1w = sbx("g1w", [128, 8, 160], BF16)
            K.dma(g1w, V("rwkv_g1", W["rwkv_g1"].ap[l].rearrange("(kc p) n -> p kc n", p=128)), q="pool")
            g2a = sbx("g2a", [128, 256], BF16)
            g2b = sbx("g2b", [32, 256], BF16)
            K.dma(g2a, V("rwkv_g2", W["rwkv_g2"].ap[l][0:128, :]), q="pool")
            K.dma(g2b, V("rwkv_g2", W["rwkv_g2"].ap[l][128:160, :]), q="pool")
            tmv = sbx("tmv", [128, 4, 256], F32)
            K.dma(tmv, tmv_in[l])
            return (win, wa1, wa2, w1, w2, a1, a2, g1w, g2a, g2b, tmv)

        def phase2(l, wts):
            K.begin_phase()
            win, wa1, wa2, w1, w2, a1, a2, g1w, g2a, g2b, tmv = wts
            hts = Rot([K.sb(f"ht{i}", [128, 8, 514], BF16) for i in range(1)])
            xx = K.sb("xx", [128, 8, 512], F32)
            xmix = K.sb("xmix", [128, 8, 512], BF16)
            pm = Rot([K.ps(f"pm{i}", [128, 512]) for i in range(5)])
            phs = K.ps("phs", [128, 16])
            phr = Rot([phs[:, 2 * i:2 * i + 2] for i in range(8)])
            psT = Rot([K.ps(f"psT{i}", [128, 512], BF16) for i in range(2)])
            qA = K.sb("qA", [128, 4, 512], BF16)
            kA = K.sb("kA", [128, 4, 512], BF16)
            gA = K.sb("gA", [128, 4, 512], BF16)
            fA = K.sb("fA", [128, 2, 512], BF16)
            ropeT = K.sb("ropeT", [128, 2, 512], F32)
            tr1 = Rot([K.sb(f"tr1_{i}", [128, 512], F32) for i in range(2)])
            tr2 = Rot([K.sb(f"tr2_{i}", [128, 512], F32) for i in range(1)])
            tm512 = Rot([K.sb(f"tm512_{i}", [128, 512], BF16) for i in range(2)])
            tm256 = Rot([K.sb(f"tm256_{i}", [128, 256], BF16) for i in range(2)])
            z1b = K.sb("z1b", [64, 512], BF16)
            zbs = Rot([K.sb(f"zb{i}", [128, 256], F32) for i in range(2)])
            la_s = Rot([K.sb(f"las{i}", [128, 256], F32) for i in range(2)])
            raws = Rot([K.sb(f"raw{i}", [128, 514], F32) for i in range(2)])
            rkvF = [K.sb(f"rkvF{i}", [128, 2, 512], F32) for i in range(3)]
            kkn = K.sb("kkn", [128, 2, 512], F32)
            sqb = Rot([K.sb(f"sqb{i}", [128, 512], BF16) for i in range(2)])
            rsr = Rot([K.sb(f"rsr{i}", [128, 512], F32) for i in range(1)])
            twb = K.sb("twb", [128, 512], BF16)
            a1b = K.sb("a1b", [128, 512], BF16)
            sg1 = K.sb("sg1", [128, 512], BF16)
            sg2 = K.sb("sg2", [32, 512], BF16)
            aF = [K.sb(f"aF{i}", [128, 2, 512], F32) for i in range(2)]
            rB = K.sb("rB", [128, 2, 512], BF16)
            aB = K.sb("aB", [128, 2, 512], BF16)
            gB = K.sb("gB", [128, 2, 512], BF16)
            vB = K.sb("vB", [128, 2, 512], BF16)
            bonB = K.sb("bonB", [128, 2, 512], BF16)
            bB = [K.sb(f"bB{i}", [128, 2, 512], BF16) for i in range(2)]
            kdB = [K.sb(f"kdB{i}", [128, 2, 512], BF16) for i in range(2)]
            tmpk = Rot([K.sb(f"tmpk{i}", [128, 512], F32) for i in range(1)])
            kdf = Rot([K.sb(f"kdf{i}", [128, 512], F32) for i in range(2)])
            rkd = Rot([K.sb(f"rkd{i}", [128, 512], BF16) for i in range(2)])
            idb = cmb[:, IDENT, :]
            obd = cmb[:, ONESBD, :]

            def fm_mm(ps, wt, c0_, rhs, Wd):
                for kc in range(8):
                    K.mm(ps[:, 0:Wd], wt[:, kc, c0_:c0_ + 128], rhs[:, kc, :], start=(kc == 0), stop=(kc == 7))

            for (seg, tok0, Wd) in cfg.tiles():
                c0 = cfg.col(tok0)
                nsub = Wd // 128
                ht = hts.next()
                K.dma(ht[:, :, 0:Wd + 2], hT_d[:, :, c0 - 1:c0 + Wd + 1])
                hc = ht[:, :, 1:Wd + 1]
                if seg == 1:
                    K.dma(ropeT[:, :, 0:Wd], rope_in[:, :, tok0 - TC:tok0 - TC + Wd])
                K.tt("dve", xx[:, :, 0:Wd], ht[:, :, 0:Wd], ht[:, :, 2:Wd + 2], ALU.add)
                K.stt("dve", xx[:, :, 0:Wd], xx[:, :, 0:Wd], 0.5, hc, ALU.mult, ALU.subtract)

                def mix(j):
                    for fc in range(8):
                        K.stt("dve", xmix[:, fc, 0:Wd], xx[:, fc, 0:Wd],
                              pvc(l, "mu", j * 8 + fc), ht[:, fc, 1:Wd + 1], ALU.mult, ALU.add)
                    return xmix
                for c in range(2):
                    p = pm.next()
                    fm_mm(p, win, 0 * 256 + c * 128, hc, Wd)
                    K.act(qA[:, c, 0:Wd], p[:, 0:Wd], AF.Copy, scale=0.125)
                    p = pm.next()
                    fm_mm(p, win, 1 * 256 + c * 128, hc, Wd)
                    K.copy("act", kA[:, c, 0:Wd], p[:, 0:Wd])
                    p = pm.next()
                    fm_mm(p, win, 3 * 256 + c * 128, hc, Wd)
                    K.act(gA[:, c, 0:Wd], p[:, 0:Wd], AF.Silu)
                    p = pm.next()
                    fm_mm(p, win, 7 * 256 + c * 128, hc, Wd)
                    K.act(gA[:, 2 + c, 0:Wd], p[:, 0:Wd], AF.Silu)
                    p = pm.next()
                    fm_mm(p, win, 11 * 256 + c * 128, hc, Wd)
                    K.copy("act", fA[:, c, 0:Wd], p[:, 0:Wd])
                    for (part, swp, dst, scl) in ((4, 3072, qA, 0.125), (5, 3328, kA, 1.0)):
                        p = pm.next()
                        fm_mm(p, win, part * 256 + c * 128, hc, Wd)
                        if seg == 0:
                            K.act(dst[:, 2 + c, 0:Wd], p[:, 0:Wd], AF.Copy, scale=scl)
                        else:
                            p2 = pm.next()
                            fm_mm(p2, win, swp + c * 128, hc, Wd)
                            ta, tb = tr1.next(), tr2.next()
                            K.stt("dve", ta[:, 0:Wd], p[:, 0:Wd], scl, ropeT[:, 0, 0:Wd], ALU.mult, ALU.mult)
                            K.stt("dve", tb[:, 0:Wd], p2[:, 0:Wd], scl, ropeT[:, 1, 0:Wd], ALU.mult, ALU.mult)
                            K.tt("dve", dst[:, 2 + c, 0:Wd], ta[:, 0:Wd], tb[:, 0:Wd], ALU.add)
                K.dma(A_qT[:, :, tok0:tok0 + Wd], qA[:, :, 0:Wd])
                K.dma(A_kT[:, :, tok0:tok0 + Wd], kA[:, :, 0:Wd])
                K.dma(A_gT[:, :, tok0:tok0 + Wd], gA[:, :, 0:Wd])
                K.dma(F_fT[:, :, tok0:tok0 + Wd], fA[:, :, 0:Wd])
                for j in range(nsub):
                    rows = slice(tok0 + j * 128, tok0 + (j + 1) * 128)
                    pt = psT.next()
                    for hp in range(4):
                        K.tr(pt[:, hp * 128:(hp + 1) * 128], kA[:, hp, j * 128:(j + 1) * 128], idb)
                    st_ = tm512.next()
                    K.copy("dve", st_, pt)
                    K.dma(A_k[rows, :].k(("A_k", tok0, j)), st_)
                    p = pm.next()
                    for hi, part in enumerate((2, 6)):
                        for kc in range(8):
                            K.mm(p[:, hi * 256:(hi + 1) * 256], ht[:, kc, 1 + j * 128:1 + (j + 1) * 128],
                                 win[:, kc, part * 256:(part + 1) * 256], start=(kc == 0), stop=(kc == 7))
                    st_ = tm512.next()
                    K.copy("act", st_, p)
                    K.dma(A_v[rows, :].k(("A_v", tok0, j)), st_)
                p = pm.next()
                for kc in range(8):
                    K.mm(p[0:64, 0:Wd], wa1[:, kc, :], ht[:, kc, 1:Wd + 1], start=(kc == 0), stop=(kc == 7))
                K.copy("act", z1b[:, 0:Wd], p[0:64, 0:Wd])
                for j in range(nsub):
                    rows = slice(tok0 + j * 128, tok0 + (j + 1) * 128)
                    for dr in range(2):
                        p = pm.next()
                        K.mm(p[:, 0:256], z1b[dr * 32:dr * 32 + 16, j * 128:(j + 1) * 128], wa2[dr * 32:dr * 32 + 16, :])
                        zb = zbs.next()
                        K.tt("dve", zb, p[:, 0:256], tmv[:, dr, :], ALU.add)
                        K.act(zb, zb, AF.Exp, scale=-1.0)
                        K.act(zb, zb, AF.Ln, bias=1.0)
                        ls = la_s.next()
                        K.act(ls, zb, AF.Copy, scale=-1.0 / 16.0)
                        K.dma(A_la[dr][rows, :].k((f"A_la{dr}", tok0, j)), ls)
                for X, part in ((0, 8), (1, 9), (2, 10)):
                    for c in range(2):
                        p = pm.next()
                        cc0 = part * 256 + c * 128
                        fm_mm(p, win, cc0, hc, Wd)
                        phh = phr.next()
                        for kc in range(8):
                            K.mm(phh, win[:, kc, cc0:cc0 + 128], ht[:, kc, 0:Wd + 2:Wd + 1], start=(kc == 0), stop=(kc == 7))
                        raw = raws.next()
                        K.copy("act", raw[:, 1:Wd + 1], p[:, 0:Wd])
                        K.copy("act", raw[:, 0:Wd + 2:Wd + 1], phh)
                        dst = rkvF[X][:, c, 0:Wd]
                        K.ts("dve", dst, raw[:, 0:Wd], pvc(l, "rconv", 0 * 6 + X * 2 + c), None, ALU.mult)
                        K.stt("dve", dst, raw[:, 1:Wd + 1], pvc(l, "rconv", 1 * 6 + X * 2 + c), dst, ALU.mult, ALU.add)
                        K.stt("dve", dst, raw[:, 2:Wd + 2], pvc(l, "rconv", 2 * 6 + X * 2 + c), dst, ALU.mult, ALU.add)
                rF, kF, vF = rkvF
                for c in range(2):
                    K.act(kkn[:, c, 0:Wd], kF[:, c, 0:Wd], AF.Identity, scale=pvc(l, "kkw", c), bias=0.0)
                    sq_ = sqb.next()
                    K.act(sq_[:, 0:Wd], kkn[:, c, 0:Wd], AF.Square)
                    p = pm.next()
                    K.mm(p[:, 0:Wd], obd, sq_[:, 0:Wd])
                    rs_ = rsr.next()
                    K.act(rs_[:, 0:Wd], p[:, 0:Wd], AF.Sqrt, bias=epsc)
                    K.recip(rs_[:, 0:Wd], rs_[:, 0:Wd])
                    K.tt("dve", kkn[:, c, 0:Wd], kkn[:, c, 0:Wd], rs_[:, 0:Wd], ALU.mult)
                    K.act(aB[:, c, 0:Wd], kkn[:, c, 0:Wd], AF.Copy, scale=-1.0)
                    K.copy("act", rB[:, c, 0:Wd], rF[:, c, 0:Wd])
                    K.copy("act", vB[:, c, 0:Wd], vF[:, c, 0:Wd])
                p = pm.next()
                xw = mix(0)
                fm_mm(p, w1, 0, xw[:, :, 0:Wd], Wd)
                K.act(twb[:, 0:Wd], p[:, 0:Wd], AF.Tanh)
                p = pm.next()
                xa = mix(1)
                fm_mm(p, a1, 0, xa[:, :, 0:Wd], Wd)
                K.copy("act", a1b[:, 0:Wd], p[:, 0:Wd])
                p = pm.next()
                xg = mix(2)
                fm_mm(p, g1w, 0, xg[:, :, 0:Wd], Wd)
                K.act(sg1[:, 0:Wd], p[:, 0:Wd], AF.Sigmoid)
                p = pm.next()
                for kc in range(8):
                    K.mm(p[0:32, 0:Wd], g1w[:, kc, 128:160], xg[:, kc, 0:Wd], start=(kc == 0), stop=(kc == 7))
                K.act(sg2[:, 0:Wd], p[0:32, 0:Wd], AF.Sigmoid)
                for c in range(2):
                    p = pm.next()
                    K.mm(p[:, 0:Wd], g2a[:, c * 128:(c + 1) * 128], sg1[:, 0:Wd], start=True, stop=False)
                    K.mm(p[:, 0:Wd], g2b[:, c * 128:(c + 1) * 128], sg2[:, 0:Wd], start=False, stop=True)
                    K.copy("act", gB[:, c, 0:Wd], p[:, 0:Wd])
                for j in range(nsub):
                    rows = slice(tok0 + j * 128, tok0 + (j + 1) * 128)
                    for dr in range(2):
                        p = pm.next()
                        K.mm(p[:, 0:256], twb[dr * 64:(dr + 1) * 64, j * 128:(j + 1) * 128], w2[dr * 64:(dr + 1) * 64, :])
                        zb = zbs.next()
                        K.tt("dve", zb, p[:, 0:256], tmv[:, 2 + dr, :], ALU.add)
                        K.act(zb, zb, AF.Sigmoid)
                        ls = la_s.next()
                        K.act(ls, zb, AF.Copy, scale=-math.exp(-0.5))
                        K.dma(B_lw[dr][rows, :].k((f"B_lw{dr}", tok0, j)), ls)
                for dr in range(2):
                    for c in range(2):
                        p = pm.next()
                        K.mm(p[:, 0:Wd], a2[dr * 64:(dr + 1) * 64, c * 128:(c + 1) * 128], a1b[dr * 64:(dr + 1) * 64, 0:Wd])
                        K.act(aF[dr][:, c, 0:Wd], p[:, 0:Wd], AF.Sigmoid, bias=pvc(l, "a0", dr * 2 + c))
                for c in range(2):
                    pb_ = pm.next()
                    for dr in range(2):
                        K.tt("dve", bB[dr][:, c, 0:Wd], kkn[:, c, 0:Wd], aF[dr][:, c, 0:Wd], ALU.mult)
                        tk, kd_, rk_ = tmpk.next(), kdf.next(), rkd.next()
                        K.act(tk[:, 0:Wd], aF[dr][:, c, 0:Wd], AF.Identity, scale=pvc(l, "ka", c), bias=omka[:, l, c:c + 1])
                        K.tt("dve", kd_[:, 0:Wd], kF[:, c, 0:Wd], tk[:, 0:Wd], ALU.mult)
                        K.copy("act", kdB[dr][:, c, 0:Wd], kd_[:, 0:Wd])
                        K.stt("dve", rk_[:, 0:Wd], rF[:, c, 0:Wd], pvc(l, "rk", c), kd_[:, 0:Wd], ALU.mult, ALU.mult)
                        K.mm(pb_[:, 0:Wd], obd, rk_[:, 0:Wd], start=(dr == 0), stop=(dr == 1))
                    K.tt("dve", bonB[:, c, 0:Wd], pb_[:, 0:Wd], vF[:, c, 0:Wd], ALU.mult)
                for j in range(nsub):
                    rows = slice(tok0 + j * 128, tok0 + (j + 1) * 128)
                    for ai, (src, dstd) in enumerate(((vB, B_v), (bB[0], B_b[0]), (bB[1], B_b[1]),
                                                      (kdB[0], B_kd[0]), (kdB[1], B_kd[1]))):
                        pt = psT.next()
                        for c in range(2):
                            K.tr(pt[:, c * 128:(c + 1) * 128], src[:, c, j * 128:(j + 1) * 128], idb)
                        st_ = tm256.next()
                        K.copy("act" if ai % 2 == 0 else "dve", st_, pt[:, 0:256])
                        K.dma(dstd[rows, :].k((dstd.key, tok0, j)), st_)
                for (src, dstd) in ((rB, B_rT), (aB, B_aT), (bB[0], B_bT[0]), (bB[1], B_bT[1]), (kdB[0], B_kdT[0]),
                                    (kdB[1], B_kdT[1]), (gB, B_gT), (bonB, B_bonT)):
                    K.dma(dstd[:, :, tok0:tok0 + Wd].k((dstd.key, tok0)), src[:, :, 0:Wd])
            K.end_phase()

        def headnorm_block(o_ap, centered, gn_ap, Wd, tiles, pmr, ones_f):
            oc_t, sq_t, rs_t = tiles
            if centered:
                p = pmr.next()
                K.mm(p[:, 0:Wd], ones_f, o_ap)
                K.stt("dve", oc_t[:, 0:Wd], p[:, 0:Wd], -1.0 / 64.0, o_ap, ALU.mult, ALU.add)
                src = oc_t[:, 0:Wd]
            else:
                src = o_ap
            K.act(sq_t[:, 0:Wd], src, AF.Square)
            p = pmr.next()
            K.mm(p[:, 0:Wd], ones_f, sq_t[:, 0:Wd])
            K.act(rs_t[:, 0:Wd], p[:, 0:Wd], AF.Sqrt, bias=epsc, scale=1.0 / 64.0)
            K.recip(rs_t[:, 0:Wd], rs_t[:, 0:Wd])
            K.stt("dve", oc_t[:, 0:Wd], src, gn_ap, rs_t[:, 0:Wd], ALU.mult, ALU.mult)
            return oc_t[:, 0:Wd]

        def phase3A(l):
            K.begin_phase()
            oacc = K.sb("oacc", [128, 4, NTOK], F32)
            Sst = K.sb("Sst", [128, 4, 64], F32)
            Sb = K.sb("Sb", [128, 4, 64], BF16)
            retla = K.sb("retla", [128, 2, 256], F32)
            K.dma(retla, retla_in)
            NBUF = 2
            qTs = Rot([K.sb(f"qT{i}", [128, 4, 128], BF16) for i in range(NBUF)])
            kTs = Rot([K.sb(f"kT{i}", [128, 4, 128], BF16) for i in range(NBUF)])
            kMs = Rot([K.sb(f"kM{i}", [128, 512], BF16) for i in range(NBUF)])
            vMs = Rot([K.sb(f"vM{i}", [128, 512], BF16) for i in range(NBUF)])
            las = Rot([K.sb(f"la{i}", [128, 256], F32) for i in range(NBUF)])
            Eb = K.sb("Eb", [128, 4, 128], F32)
            Enb = K.sb("Enb", [128, 4, 128], F32)
            Ec = K.sb("Ec", [128, 512], F32)
            qt = K.sb("qt", [128, 4, 128], BF16)
            kt = K.sb("kt", [128, 4, 128], BF16)
            kh = K.sb("kh", [128, 512], BF16)
            sc = K.sb("sc", [128, 8, 128], BF16)
            ps_b = K.ps("ps_b", [128, 4, 128])
            ps_c = K.ps("ps_c", [128, 512])
            ps_s = K.ps("ps_s", [128, 8, 128])
            ps_o = K.ps("ps_o", [128, 4, 128])
            ps_S = K.ps("ps_S", [128, 4, 64])
            for dr in range(2):
                K.memset("dve", Sst, 0.0)
                K.memset("pool", Sb, 0.0)
                TRI = cm[:, LE if dr == 0 else GE, :]
                AFT = cm[:, GT if dr == 0 else LT, :]
                last = 127 if dr == 0 else 0
                for ch in order_for(dr):
                    tok = ch * 128
                    qT, kT, kM, vM, la = qTs.next(), kTs.next(), kMs.next(), vMs.next(), las.next()
                    K.dma(qT, A_qT[:, :, tok:tok + 128])
                    K.dma(kT, A_kT[:, :, tok:tok + 128])
                    K.dma(kM, A_k[tok:tok + 128, :])
                    K.dma(vM, A_v[tok:tok + 128, :])
                    K.dma(la, A_la[dr][tok:tok + 128, :])
                    for hp in range(4):
                        lhs = la[:, hp * 128:(hp + 1) * 128] if hp < 2 else retla[:, dr, (hp - 2) * 128:(hp - 1) * 128]
                        K.mm(ps_b[:, hp, :], lhs, TRI)
                    K.act(Eb, ps_b, AF.Exp)
                    K.act(Enb, ps_b, AF.Exp, scale=-1.0)
                    K.mm(ps_c[:, 0:256], AFT, la)
                    K.mm(ps_c[:, 256:512], AFT, retla[:, dr, :])
                    K.act(Ec, ps_c, AF.Exp)
                    K.tt("dve", qt, qT, Eb, ALU.mult)
                    K.tt("pool", kt, kT, Enb, ALU.mult)
                    K.tt("pool", kh, kM, Ec, ALU.mult)
                    for h in range(8):
                        hp, pb = h // 2, (h % 2) * 64
                        K.mm(ps_s[:, (h % 2) * 4 + hp, :], kt[pb:pb + 64, hp, :], qt[pb:pb + 64, hp, :])
                    K.tt("dve", sc, ps_s, bc(TRI, [128, 8, 128], 1), ALU.mult)
                    for h in range(8):
                        hp, pb = h // 2, (h % 2) * 64
                        K.mm(ps_o[pb:pb + 64, hp, :], vM[:, h * 64:(h + 1) * 64], sc[:, (h % 2) * 4 + hp, :], start=True, stop=False)
                        K.mm(ps_o[pb:pb + 64, hp, :], Sb[pb:pb + 64, hp, :], qt[pb:pb + 64, hp, :], start=False, stop=True)
                    if dr == 0:
                        K.copy("act", oacc[:, :, tok:tok + 128], ps_o)
                    else:
                        K.tt("dve", oacc[:, :, tok:tok + 128], oacc[:, :, tok:tok + 128], ps_o, ALU.add)
                    for h in range(8):
                        hp, pb = h // 2, (h % 2) * 64
                        K.mm(ps_S[pb:pb + 64, hp, :], kh[:, h * 64:(h + 1) * 64], vM[:, h * 64:(h + 1) * 64])
                    dec = V(Eb.key, Eb.ap[:, :, last:last + 1].to_broadcast([128, 4, 64]))
                    K.tt("dve", Sst, Sst, dec, ALU.mult)
                    K.tt("dve", Sst, Sst, ps_S, ALU.add)
                    K.copy("act", Sb, Sst)
            ones_f = cm[:, ONESBD, :]
            pmr = Rot([K.ps("pe1", [128, 512]), K.ps("pe2", [128, 512])])
            octs = Rot([K.sb(f"oct{i}", [128, 512], F32) for i in range(2)])
            sqts = Rot([K.sb(f"sqt{i}", [128, 512], F32) for i in range(2)])
            rsts = Rot([K.sb(f"rst{i}", [128, 512], F32) for i in range(2)])
            gts = Rot([K.sb(f"gt{i}", [128, 4, 512], BF16) for i in range(2)])
            brs = Rot([K.sb(f"brs{i}", [128, 4, 512], BF16) for i in range(2)])
            for (seg, tok0, Wd) in cfg.tiles():
                gt_ = gts.next()
                br_ = brs.next()
                K.dma(gt_[:, :, 0:Wd], A_gT[:, :, tok0:tok0 + Wd])
                for hp in range(4):
                    gn = pvc(l, "gla_gn", hp) if hp < 2 else pvc(l, "ret_gn", hp - 2)
                    res_ = headnorm_block(oacc[:, hp, tok0:tok0 + Wd], hp >= 2, gn, Wd,
                                          (octs.next(), sqts.next(), rsts.next()), pmr, ones_f)
                    K.tt("pool", br_[:, hp, 0:Wd], res_, gt_[:, hp, 0:Wd], ALU.mult)
                K.dma(BR[:, 0:4, tok0:tok0 + Wd].k(("BR", "a", tok0)), br_[:, :, 0:Wd])
            K.end_phase()

        def phase3B(l, prefetch=None):
            K.begin_phase()
            pre_out = prefetch() if prefetch is not None else None
            yacc = K.sb("yacc", [128, 2, NTOK], F32)
            MSK = [K.sb(f"MSK{i}", [128, 4, 128], F32) for i in range(2)]
            for dr in range(2):
                st_m, in_m = (BLT, BLE) if dr == 0 else (BGT, BGE)
                for kind, mk in enumerate((st_m, in_m, st_m, in_m)):
                    K.copy("pool", MSK[dr][:, kind, :], cm[:, mk, :])
            B0 = K.ps("B0", [128, 512])
            B1 = K.ps("B1", [128, 512])
            PA = [K.ps(f"PA{i}", [128, 512]) for i in range(4)]
            PN = K.ps("PN", [128, 4, 128])
            B7 = K.ps("B7", [128, 512])
            T4 = [V(PA[i].key, PA[i].ap.rearrange("p (a b) -> p a b", a=4)) for i in range(4)]
            Yps = V(B1.key, B1.ap[:, 256:512].rearrange("p (a b) -> p a b", a=2))
            Hps = V(PA[0].key, PA[0].ap[:, 0:128].rearrange("p (a b) -> p a b", a=2))
            identb = bc(cmb[:, IDENT, :], [128, 4, 128], 1)

            def mkdir_(dr):
                H = K.sb("H", [128, 2, 64], F32)
                Hb = K.sb("Hb", [128, 2, 64], BF16)
                NB = 2
                rTs = Rot([K.sb(f"rT{i}", [128, 2, 128], BF16) for i in range(NB)])
                aTs = Rot([K.sb(f"aT{i}", [128, 2, 128], BF16) for i in range(NB)])
                bTs = Rot([K.sb(f"bT{i}", [128, 2, 128], BF16) for i in range(NB)])
                kdTs = Rot([K.sb(f"kdT{i}", [128, 2, 128], BF16) for i in range(NB)])
                vMs = Rot([K.sb(f"vM{i}", [128, 256], BF16) for i in range(NB)])
                bMs = Rot([K.sb(f"bM{i}", [128, 256], BF16) for i in range(NB)])
                kdMs = Rot([K.sb(f"kdM{i}", [128, 256], BF16) for i in range(NB)])
                lws = Rot([K.sb(f"lw{i}", [128, 256], F32) for i in range(NB)])
                Ei = K.sb("Ei", [128, 2, 128], F32)
                Eni = K.sb("Eni", [128, 2, 128], F32)
                Ee = K.sb("Ee", [128, 2, 128], F32)
                Ea = K.sb("Ea", [128, 256], F32)
                arz = [K.sb(f"arz{i}", [128, 2, 2, 128], BF16) for i in range(2)]
                bt = K.sb("bt", [128, 2, 128], BF16)
                kt = K.sb("kt", [128, 2, 128], BF16)
                bh = K.sb("bh", [128, 256], BF16)
                khz = [K.sb(f"khz{i}", [128, 256], BF16) for i in range(2)]
                Rbz = [K.sb(f"Rbz{i}", [128, 256], BF16) for i in range(2)]
                Ubz = [K.sb(f"Ubz{i}", [128, 256], BF16) for i in range(2)]
                AM = K.sb("AM", [128, 4, 4, 128], BF16)
                Nb = K.sb("Nb", [128, 4, 128], BF16)
                Q = K.sb("Q", [128, 4, 128], BF16)
                Qt = K.sb("Qt", [128, 4, 128], BF16)
                M2b = [K.sb(f"M2b{i}", [128, 4, 128], BF16) for i in range(2)]
                N2b = [K.sb(f"N2b{i}", [128, 4, 128], BF16) for i in range(2)]
                for t_ in arz + khz + Rbz + Ubz:
                    K.memset("pool", t_, 0.0)
                Ei2 = V(Ei.key, Ei.ap.rearrange("p a b -> p (a b)"))
                Eni2 = V(Eni.key, Eni.ap.rearrange("p a b -> p (a b)"))
                Ee2 = V(Ee.key, Ee.ap.rearrange("p a b -> p (a b)"))
                K.memset("dve", H, 0.0)
                K.memset("pool", Hb, 0.0)
                return dict(locals())

            DT = [mkdir_(0), mkdir_(1)]
            visited = set()

            def proc(dr, ch):
                g = DT[dr]
                H, Hb, rTs, aTs, bTs, kdTs, vMs, bMs, kdMs, lws, Ei, Eni, Ee, Ea, arz, bt, kt, bh, khz, Rbz, Ubz, AM, Nb, Q, Qt, M2b, N2b, Ei2, Eni2, Ee2 = (g[k_] for k_ in ['H', 'Hb', 'rTs', 'aTs', 'bTs', 'kdTs', 'vMs', 'bMs', 'kdMs', 'lws', 'Ei', 'Eni', 'Ee', 'Ea', 'arz', 'bt', 'kt', 'bh', 'khz', 'Rbz', 'Ubz', 'AM', 'Nb', 'Q', 'Qt', 'M2b', 'N2b', 'Ei2', 'Eni2', 'Ee2'])
                INC, EXC, AFT = (BLE, BLT, BGT) if dr == 0 else (BGE, BGT, BLT)
                tok = ch * 128
                rT, aT, bT, kdT = rTs.next(), aTs.next(), bTs.next(), kdTs.next()
                vM, bM, kdM, lw = vMs.next(), bMs.next(), kdMs.next(), lws.next()
                K.dma(rT, B_rT[:, :, tok:tok + 128])
                K.dma(aT, B_aT[:, :, tok:tok + 128])
                K.dma(bT, B_bT[dr][:, :, tok:tok + 128])
                K.dma(kdT, B_kdT[dr][:, :, tok:tok + 128])
                K.dma(vM, B_v[tok:tok + 128, :])
                K.dma(bM, B_b[dr][tok:tok + 128, :])
                K.dma(kdM, B_kd[dr][tok:tok + 128, :])
                K.dma(lw, B_lw[dr][tok:tok + 128, :])
                for hp in range(2):
                    K.mm(B0[:, hp * 128:(hp + 1) * 128], lw[:, hp * 128:(hp + 1) * 128], cm[:, INC, :])
                    K.mm(B0[:, 256 + hp * 128:256 + (hp + 1) * 128], lw[:, hp * 128:(hp + 1) * 128], cm[:, EXC, :])
                K.mm(B1[:, 0:256], cm[:, AFT, :], lw)
                K.act(Ei2, B0[:, 0:256], AF.Exp)
                K.act(Eni2, B0[:, 0:256], AF.Exp, scale=-1.0)
                K.act(Ee2, B0[:, 256:512], AF.Exp)
                K.act(Ea, B1[:, 0:256], AF.Exp)
                for par in range(2):
                    pr = slice(par * 64, par * 64 + 64)
                    K.tt("dve" if par == 0 else "pool", arz[par][pr, :, 0, :], aT[pr, :, :], Ee[pr, :, :], ALU.mult)
                    K.tt("pool" if par == 0 else "dve", arz[par][pr, :, 1, :], rT[pr, :, :], Ei[pr, :, :], ALU.mult)
                K.tt("dve", bt, bT, Eni, ALU.mult)
                K.tt("dve", kt, kdT, Eni, ALU.mult)
                K.tt("dve", bh, bM, Ea, ALU.mult)
                for half in range(2):
                    hs = slice(64 * half, 64 * half + 64)
                    K.tt("dve" if half == 0 else "pool", khz[half][hs, :], kdM[hs, :], Ea[hs, :], ALU.mult)
                for h in range(4):
                    hp, z = h // 2, arz[h % 2]
                    K.mm(PA[h][:, 0:256], bt[:, hp, :], z[:, hp, :, :])
                    K.mm(PA[h][:, 256:512], kt[:, hp, :], z[:, hp, :, :])
                    K.mm(PN[:, h, :], z[:, hp, 0, :], bt[:, hp, :])
                for h in range(4):
                    K.tt("dve", V(AM.key, AM.ap[:, h, :, :].rearrange("p a b -> p (a b)")), PA[h],
                         V(MSK[dr].key, MSK[dr].ap.rearrange("p a b -> p (a b)")), ALU.mult)
                K.tt("dve", Nb, PN, bc(cm[:, AFT, :], [128, 4, 128], 1), ALU.mult)
                Mk = AM[:, :, 0, :]
                Nk = Nb
                K.tt("pool", Q, Mk, identb, ALU.add)
                for lev in range(5):
                    for h in range(4):
                        if lev < 4:
                            K.mm(T4[0][:, h, :], Nk[:, h, :], Mk[:, h, :])
                        K.mm(T4[1][:, h, :], Mk[:, h, :], Nk[:, h, :])
                    Mn, Nn = M2b[lev % 2], N2b[lev % 2]
                    if lev < 4:
                        K.copy("act", Mn, T4[0])
                    K.copy("dve", Nn, T4[1])
                    for h in range(4):
                        K.mm(T4[2][:, h, :], Nn[:, h, :], Q[:, h, :])
                    K.tt("dve", Q, Q, T4[2], ALU.add)
                    Mk, Nk = Mn, Nn
                for half in ((0, 1) if dr == 0 else (1, 0)):
                    hs = slice(64 * half, 64 * half + 64)
                    Rb, Ub = Rbz[half], Ubz[half]
                    for h in range(4):
                        hp, z = h // 2, arz[h % 2]
                        K.mm(B7[hs, h * 64:(h + 1) * 64], z[:, hp, 0, hs], Hb[:, hp, :], start=True, stop=False)
                        K.mm(B7[hs, h * 64:(h + 1) * 64], AM[:, h, 2, hs], vM[:, h * 64:(h + 1) * 64], start=False, stop=True)
                    K.copy("act", Rb[hs, :], B7[hs, 0:256])
                    for h in range(4):
                        K.mm(B7[hs, 256 + h * 64:256 + (h + 1) * 64], Q[:, h, hs], Rb[:, h * 64:(h + 1) * 64])
                    K.copy("dve", Ub[hs, :], B7[hs, 256:512])
                    for h in range(4):
                        hp, pb, z = h // 2, (h % 2) * 64, arz[h % 2]
                        o_ = Yps[pb:pb + 64, hp, hs]
                        K.mm(o_, Hb[:, hp, :], z[:, hp, 1, hs], start=True, stop=False)
                        K.mm(o_, Ub[:, h * 64:(h + 1) * 64], AM[:, h, 1, hs], start=False, stop=False)
                        K.mm(o_, vM[:, h * 64:(h + 1) * 64], AM[:, h, 3, hs], start=False, stop=True)
                    for h in range(4):
                        hp, pb = h // 2, (h % 2) * 64
                        o_ = Hps[pb:pb + 64, hp, :]
                        K.mm(o_, bh[:, h * 64:(h + 1) * 64], Ub[:, h * 64:(h + 1) * 64], start=True, stop=False)
                        K.mm(o_, khz[half][:, h * 64:(h + 1) * 64], vM[:, h * 64:(h + 1) * 64], start=False, stop=True)
                    endc = 64 * half + (63 if dr == 0 else 0)
                    dec = V(Ei.key, Ei.ap[:, :, endc:endc + 1].to_broadcast([128, 2, 64]))
                    K.tt("dve", H, H, dec, ALU.mult)
                    K.tt("dve", H, H, Hps, ALU.add)
                    K.copy("act", Hb, H)
                if ch not in visited:
                    visited.add(ch)
                    K.copy("act", yacc[:, :, tok:tok + 128], Yps)
                else:
                    K.tt("dve", yacc[:, :, tok:tok + 128], yacc[:, :, tok:tok + 128], Yps, ALU.add)

            orders = [order_for(0), order_for(1)]
            for k_ in range(NCH):
                for dr in range(2):
                    proc(dr, orders[dr][k_])
            ones_f = cm[:, ONESBD, :]
            pmr = Rot([B0, B7])
            octs = Rot([K.sb(f"oct{i}", [128, 512], F32) for i in range(2)])
            sqts = Rot([K.sb(f"sqt{i}", [128, 512], F32) for i in range(2)])
            rsts = Rot([K.sb(f"rst{i}", [128, 512], F32) for i in range(2)])
            gts = Rot([K.sb(f"gt{i}", [128, 2, 512], BF16) for i in range(2)])
            bns = Rot([K.sb(f"bn{i}", [128, 2, 512], BF16) for i in range(2)])
            brs = Rot([K.sb(f"brs{i}", [128, 2, 512], BF16) for i in range(2)])
            for (seg, tok0, Wd) in cfg.tiles():
                gt_, bn_, br_ = gts.next(), bns.next(), brs.next()
                K.dma(gt_[:, :, 0:Wd], B_gT[:, :, tok0:tok0 + Wd])
                K.dma(bn_[:, :, 0:Wd], B_bonT[:, :, tok0:tok0 + Wd])
                for c in range(2):
                    res_ = headnorm_block(yacc[:, c, tok0:tok0 + Wd], True, pvc(l, "rwkv_gn", c), Wd,
                                          (octs.next(), sqts.next(), rsts.next()), pmr, ones_f)
                    K.tt("pool", res_, res_, bn_[:, c, 0:Wd], ALU.add)
                    K.tt("pool", br_[:, c, 0:Wd], res_, gt_[:, c, 0:Wd], ALU.mult)
                K.dma(BR[:, 4:6, tok0:tok0 + Wd].k(("BR", "b", tok0)), br_[:, :, 0:Wd])
            K.end_phase()
            return pre_out

        def phase3F(l):
            K.begin_phase()
            fT = K.sb("fT", [128, 2, NTOK], BF16)
            K.dma(fT, F_fT)
            d64 = K.sb("d64", [128, 2, 128], BF16)
            K.dma(d64, dft64_in, q="pool")
            G = K.sb("G", [128, NCH, 2, 256], BF16)
            pg = Rot([K.ps(f"pg{i}", [128, 512]) for i in range(2)])
            po = Rot([K.ps(f"po{i}", [128, 512]) for i in range(3)])
            for ch in range(NCH):
                p = pg.next()
                for k in range(2):
                    for c in range(2):
                        K.mm(p[:, k * 256 + c * 128:k * 256 + (c + 1) * 128], fT[:, c, ch * 128:(ch + 1) * 128], d64[:, k, :])
                K.copy("act", G[:, ch, 0, :], p[:, 0:256])
                K.ts("dve", G[:, ch, 1, :], p[:, 256:512], -1.0, None, ALU.mult)
            maxch = max(2, NCH - 2)
            tabr = Rot([K.sb(f"tab{i}", [128, maxch, 512], BF16) for i in range(3)])
            obr = Rot([K.sb(f"ob{i}", [128, 512], BF16) for i in range(2)])
            for (sname, ch0, nch, T, tab_in) in (("c", 0, 2, TC, dftC_in), ("l", 2, NCH - 2, TL, dftL_in)):
                for n0 in range(0, T, 512):
                    nw = min(512, T - n0)
                    tabs = []
                    for k in range(2):
                        tb = tabr.next()
                        src = tab_in.ap[k].rearrange("(ch p) n -> p ch n", p=128)
                        for j0 in range(0, nch, 8):
                            j1 = min(nch, j0 + 8)
                            K.dma(tb[:, j0:j1, 0:nw], V("dft", src[:, j0:j1, n0:n0 + nw]))
                        tabs.append(tb)
                    for c in range(2):
                        p = po.next()
                        first = True
                        for k in range(2):
                            for j in range(nch):
                                K.mm(p[:, 0:nw], G[:, ch0 + j, k, c * 128:(c + 1) * 128], tabs[k][:, j, 0:nw],
                                     start=first, stop=(k == 1 and j == nch - 1))
                                first = False
                        ob = obr.next()
                        K.copy("act", ob[:, 0:nw], p[:, 0:nw])
                        K.dma(BR[:, 6 + c, ch0 * 128 + n0:ch0 * 128 + n0 + nw].k(("BR", "f", c, n0, sname)), ob[:, 0:nw])
            K.end_phase()

        def tm_bcast(dst, l, base, pz, onesf, diags):
            for j in range(2):
                for fc in range(8):
                    d_ = diags.next()
                    K.ts("dve", d_, cm[:, IDENT, :], mod[:, l, base + fc, j:j + 1], None, ALU.mult)
                    p = pz.next()
                    K.mm(p[:, 0:128], onesf, d_)
                    K.copy("act", dst[:, j, fc * 128:(fc + 1) * 128], p[:, 0:128])

        def p4_weights(l):
            wg = K.sb("wg", [128, 8, 4096], BF16, x=True)
            for i in range(4):
                src = W["w_gate"].ap[l, i].rearrange("(kc p) n -> p kc n", p=128)
                for kc in range(8):
                    K.dma(wg[:, kc, i * 1024:(i + 1) * 1024], V("w_gate", src[:, kc, :]), q="pool")
            return wg

        def phase4(l, last, wg):
            K.begin_phase()
            wbr = K.sb("wbr", [128, 8, 1024], BF16)
            for i in range(4):
                K.dma(wbr[:, i * 2:(i + 1) * 2, :], V("w_br", W["w_br"].ap[l, i].rearrange("(c p) n -> p c n", p=128)), q="pool")
            wo = K.sb("wo", [128, 8, 1024], BF16)
            srco = W["w_out"].ap[l].rearrange("(kc p) n -> p kc n", p=128)
            for kc in range(8):
                K.dma(wo[:, kc, :], V("w_out", srco[:, kc, :]), q="pool")
            onesf = K.sb("onesf", [128, 128], F32)
            K.memset("dve", onesf, 1.0)
            m2tm = K.sb("m2tm", [128, 2, 1024], F32)
            pz = Rot([K.ps(f"pz{i}", [128, 512]) for i in range(6)])
            diags = Rot([K.sb(f"diag{i}", [128, 128], F32) for i in range(2)])
            tm_bcast(m2tm, l, 16, pz, onesf, diags)
            ht = K.sb("ht4", [128, 8, 512], BF16)
            brt = K.sb("brt", [128, 8, 512], BF16)
            zT = K.sb("zT", [128, 8, 512], BF16)
            gsr = Rot([K.sb(f"gs{i}", [128, 512], F32) for i in range(3)])
            zaccs = Rot([K.sb(f"zacc{i}", [128, 512], F32) for i in range(2)])
            xts = Rot([K.sb(f"xt4_{i}", [128, D], F32) for i in range(2)])
            tmps = Rot([K.sb(f"tmp4_{i}", [128, 512], F32) for i in range(2)])
            sq = K.sb("sq4", [128, D], F32)
            ss = K.sb("ss4", [128, 1], F32)
            rs = K.sb("rs4", [128, 1], F32)
            xn = K.sb("xn4", [128, D], F32)
            pT = K.ps("pT4", [128, 8, 128])
            hsb = K.sb("hsb4", [128, 8, 128], BF16)
            for (seg, tok0, Wd) in cfg.tiles():
                if last and seg == 0:
                    continue
                jm = 1 - seg
                c0 = cfg.col(tok0)
                K.dma(ht[:, :, 0:Wd], hT_d[:, :, c0:c0 + Wd])
                K.dma(brt[:, :, 0:Wd], BR[:, :, tok0:tok0 + Wd])
                for oc in range(8):
                    zacc = zaccs.next()
                    for i in range(4):
                        pgt = pz.next()
                        for kc in range(8):
                            K.mm(pgt[:, 0:Wd], wg[:, kc, i * 1024 + oc * 128:i * 1024 + (oc + 1) * 128], ht[:, kc, 0:Wd],
                                 start=(kc == 0), stop=(kc == 7))
                        gs = gsr.next()
                        K.act(gs[:, 0:Wd], pgt[:, 0:Wd], AF.Sigmoid, bias=pvc(l, "b_gate", i * 8 + oc))
                        pb_ = pz.next()
                        for c in range(2):
                            K.mm(pb_[:, 0:Wd], wbr[:, i * 2 + c, oc * 128:(oc + 1) * 128], brt[:, i * 2 + c, 0:Wd],
                                 start=(c == 0), stop=(c == 1))
                        if i == 0:
                            K.tt("dve", zacc[:, 0:Wd], gs[:, 0:Wd], pb_[:, 0:Wd], ALU.mult)
                        else:
                            K.tt("dve", gs[:, 0:Wd], gs[:, 0:Wd], pb_[:, 0:Wd], ALU.mult)
                            K.tt("dve", zacc[:, 0:Wd], zacc[:, 0:Wd], gs[:, 0:Wd], ALU.add)
                    K.copy("act", zT[:, oc, 0:Wd], zacc[:, 0:Wd])
                for j in range(Wd // 128):
                    st = tok0 // 128 + j
                    xt = xts.next()
                    K.dma(xt, x_src(l, st))
                    for half in range(2):
                        p = pz.next()
                        for oc in range(8):
                            K.mm(p, zT[:, oc, j * 128:(j + 1) * 128], wo[:, oc, half * 512:(half + 1) * 512],
                                 start=(oc == 0), stop=(oc == 7))
                        tmp = tmps.next()
                        K.tt("dve", tmp, p, m2tm[:, jm, half * 512:(half + 1) * 512], ALU.mult)
                        K.tt("dve", xt[:, half * 512:(half + 1) * 512], xt[:, half * 512:(half + 1) * 512], tmp, ALU.add)
                    K.dma(xres[st * 128:(st + 1) * 128, :].k(("xres", st)), xt)
                    norm_to_fm(xt, lambda fc: sc2[:, l, fc, jm:jm + 1], lambda fc: mod[:, l, 24 + fc, jm:jm + 1], seg,
                               h2T_d, cfg.col(st * 128), (sq, ss, rs, xn, pT, hsb))
            K.end_phase()

        def phase5(l, last):
            K.begin_phase()
            srcu = W["ffn_up"].ap[l].rearrange("(kc p) n -> p kc n", p=128)
            NC_ = FF // 128
            wupA = K.sb("wupA", [128, 8, FF], BF16)
            wupU = K.sb("wupU", [128, 8, FF], BF16)
            for kc in range(8):
                K.dma(wupA[:, kc, :], V("ffn_up", srcu[:, kc, 0:FF]), q="pool")
                K.dma(wupU[:, kc, :], V("ffn_up", srcu[:, kc, FF:2 * FF]), q="pool")

            def wsel(t_, cc):
                return t_, cc * 128
            wdn = K.sb("wdn", [128, NC_, D], BF16)
            srcd = W["ffn_down"].ap[l].rearrange("(c p) n -> p c n", p=128)
            for c_ in range(0, NC_, 4):
                c1 = min(NC_, c_ + 4)
                K.dma(wdn[:, c_:c1, :], V("ffn_down", srcd[:, c_:c1, :]), q="pool")
            onesf = K.sb("onesf5", [128, 128], F32)
            K.memset("dve", onesf, 1.0)
            m5tm = K.sb("m5tm", [128, 2, 1024], F32)
            pz = Rot([K.ps(f"pz5_{i}", [128, 512]) for i in range(6)])
            phs = K.ps("phs5", [128, 16])
            phr = Rot([phs[:, 2 * i:2 * i + 2] for i in range(8)])
            diags = Rot([K.sb(f"diag5_{i}", [128, 128], F32) for i in range(2)])
            tm_bcast(m5tm, l, 40, pz, onesf, diags)
            h2t = K.sb("h2t", [128, 8, 514], BF16)
            gt = K.sb("gt5", [128, NC_, 512], BF16)
            cvs = Rot([K.sb(f"cv5_{i}", [128, 512], F32) for i in range(3)])
            xts = Rot([K.sb(f"xt5_{i}", [128, D], F32) for i in range(2)])
            tmps = Rot([K.sb(f"tmp5_{i}", [128, 512], F32) for i in range(2)])
            for (seg, tok0, Wd) in cfg.tiles():
                if last and seg == 0:
                    continue
                jm = 1 - seg
                c0 = cfg.col(tok0)
                K.dma(h2t[:, :, 0:Wd + 2], h2T_d[:, :, c0 - 1:c0 + Wd + 1])
                for cc in range(NC_):
                    pa = pz.next()
                    wa_, oa_ = wsel(wupA, cc)
                    wu_, ou_ = wsel(wupU, cc)
                    for kc in range(8):
                        K.mm(pa[:, 0:Wd], wa_[:, kc, oa_:oa_ + 128], h2t[:, kc, 1:Wd + 1], start=(kc == 0), stop=(kc == 7))
                    phh = phr.next()
                    for kc in range(8):
                        K.mm(phh, wa_[:, kc, oa_:oa_ + 128], h2t[:, kc, 0:Wd + 2:Wd + 1], start=(kc == 0), stop=(kc == 7))
                    pu = pz.next()
                    for kc in range(8):
                        K.mm(pu[:, 0:Wd], wu_[:, kc, ou_:ou_ + 128], h2t[:, kc, 1:Wd + 1],
                             start=(kc == 0), stop=(kc == 7))
                    cv = cvs.next()
                    w0_, w1_, w2_ = (pvc(l, "fconv", j_ * NC_ + cc) for j_ in range(3))
                    K.act(cv[:, 0:Wd], pa[:, 0:Wd], AF.Identity, scale=w1_, bias=0.0)
                    K.stt("dve", cv[:, 1:Wd], pa[:, 0:Wd - 1], w0_, cv[:, 1:Wd], ALU.mult, ALU.add)
                    K.stt("dve", cv[:, 0:Wd - 1], pa[:, 1:Wd], w2_, cv[:, 0:Wd - 1], ALU.mult, ALU.add)
                    K.stt("dve", cv[:, 0:1], phh[:, 0:1], w0_, cv[:, 0:1], ALU.mult, ALU.add)
                    K.stt("dve", cv[:, Wd - 1:Wd], phh[:, 1:2], w2_, cv[:, Wd - 1:Wd], ALU.mult, ALU.add)
                    K.act(cv[:, 0:Wd], cv[:, 0:Wd], AF.Silu, bias=pvc(l, "fconvb", cc))
                    K.tt("dve", gt[:, cc, 0:Wd], cv[:, 0:Wd], pu[:, 0:Wd], ALU.mult)
                for j in range(Wd // 128):
                    st = tok0 // 128 + j
                    xt = xts.next()
                    K.dma(xt, xres[st * 128:(st + 1) * 128, :].k(("xres", st)))
                    for half in range(2):
                        p = pz.next()
                        for cc in range(NC_):
                            K.mm(p, gt[:, cc, j * 128:(j + 1) * 128], wdn[:, cc, half * 512:(half + 1) * 512],
                                 start=(cc == 0), stop=(cc == NC_ - 1))
                        tmp = tmps.next()
                        K.tt("dve", tmp, p, m5tm[:, jm, half * 512:(half + 1) * 512], ALU.mult)
                        K.tt("dve", xt[:, half * 512:(half + 1) * 512], xt[:, half * 512:(half + 1) * 512], tmp, ALU.add)
                    K.dma(xres[st * 128:(st + 1) * 128, :].k(("xres", st)), xt)
            K.end_phase()

        def phase6():
            K.begin_phase()
            gf = K.sb("gf", [128, D], F32)
            K.dma(gf, gfin_in)
            xts = Rot([K.sb(f"xt6_{i}", [128, D], F32) for i in range(2)])
            sqs = Rot([K.sb(f"sq6_{i}", [128, D], F32) for i in range(2)])
            sss = Rot([K.sb(f"ss6_{i}", [128, 1], F32) for i in range(2)])
            ots = Rot([K.sb(f"ot6_{i}", [128, D], F32) for i in range(2)])
            for st in range(2, NCH):
                xt, sq, ss, ot = xts.next(), sqs.next(), sss.next(), ots.next()
                K.dma(xt, xres[st * 128:(st + 1) * 128, :].k(("xres", st)))
                K.act(sq, xt, AF.Square, accum=ss)
                K.act(ss, ss, AF.Sqrt, bias=epsc, scale=1.0 / D)
                K.recip(ss, ss)
                K.stt("dve", ot, xt, ss, gf, ALU.mult, ALU.mult)
                K.dma(out_d[(st - 2) * 128:(st - 1) * 128, :].k(("out", st)), ot, is_output=True)
            K.end_phase()

        for l in range(L):
            last = l == L - 1
            K.xbegin()
            K.begin_phase()
            wts2 = p2_weights(l)
            NB = 3
            xts = [K.sb(f"xt{i}", [128, D], F32) for i in range(NB)]
            sqs = [K.sb(f"sq{i}", [128, D], F32) for i in range(NB)]
            sss = [K.sb(f"ss{i}", [128, 1], F32) for i in range(NB)]
            rss = [K.sb(f"rs{i}", [128, 1], F32) for i in range(NB)]
            xns = [K.sb(f"xn{i}", [128, D], F32) for i in range(NB)]
            pTs = [K.ps(f"pT{i}", [128, 8, 128]) for i in range(NB)]
            hss = [K.sb(f"hs{i}", [128, 8, 128], BF16) for i in range(NB)]
            for st in range(NCH):
                i = st % NB
                seg = 0 if st < 2 else 1
                j = 1 - seg
                K.dma(xts[i], x_src(l, st))
                norm_to_fm(xts[i], lambda fc: sc1[:, l, fc, j:j + 1], lambda fc: mod[:, l, fc, j:j + 1], seg,
                           hT_d, cfg.col(st * 128), (sqs[i], sss[i], rss[i], xns[i], pTs[i], hss[i]))
            K.end_phase()
            if cfg.stop_after == ("p1", l):
                break
            phase2(l, wts2)
            K.xend()
            phase3F(l)
            phase3A(l)
            K.xbegin()
            wg_ = phase3B(l, prefetch=lambda: p4_weights(l))
            phase4(l, last, wg_)
            K.xend()
            if cfg.stop_after == ("p4", l):
                break
            phase5(l, last)
            if cfg.stop_after == ("p5", l):
                break
        if cfg.stop_after is None:
            phase6()

    return nc


def _fm_cols(v):
    v = np.asarray(v, np.float32)
    return np.ascontiguousarray(v.reshape(-1, 128).T)


def host_consts(TL):
    i = np.arange(128)
    jj, ii = np.meshgrid(i, i, indexing="ij")
    same = (jj // 64) == (ii // 64)
    mats = [np.eye(128), same.astype(np.float64), jj <= ii, jj < ii, jj >= ii, jj > ii,
            same & (jj <= ii), same & (jj < ii), same & (jj >= ii), same & (jj > ii)]
    cmask = np.stack([m.astype(np.float32) for m in mats], axis=1)
    t = np.arange(TL)
    row = (t // 64).astype(np.float32)
    colv = (t % 64).astype(np.float32)
    nfreq = 16
    inv = (np.float32(10000.0) ** (-np.arange(nfreq, dtype=np.float32) / nfreq)).astype(np.float32)
    ang = np.concatenate([row[:, None] * inv, colv[:, None] * inv], axis=-1).astype(np.float32)
    cosT, sinT = np.cos(ang), np.sin(ang)
    rope = np.zeros((128, 2, TL), np.float32)
    for p in range(128):
        d = p % 64
        rope[p, 0] = cosT[:, d % 32]
        rope[p, 1] = -sinT[:, d % 32] if d < 32 else sinT[:, d % 32]
    retla = np.zeros((128, 2, 256), np.float32)
    for dr in range(2):
        expo = -5.0 - np.arange(4, dtype=np.float32)
        if dr == 1:
            expo = expo[::-1]
        lg = np.log1p(-np.exp2(expo)).astype(np.float32)
        retla[:, dr, :] = np.repeat(lg, 64)[None, :]

    def dft(T):
        k = np.arange(T, dtype=np.int64)
        m = (k[:, None] * k[None, :]) % T
        a = 2.0 * np.pi * m.astype(np.float64) / T
        s = 1.0 / np.sqrt(T)
        return np.stack([np.cos(a) * s, np.sin(a) * s]).astype(ml_dtypes.bfloat16)

    d64 = np.zeros((128, 2, 128), np.float32)
    k = np.arange(64)
    a = 2.0 * np.pi * ((k[:, None] * k[None, :]) % 64) / 64.0
    for hb in range(2):
        d64[hb * 64:(hb + 1) * 64, 0, hb * 64:(hb + 1) * 64] = np.cos(a) / 8.0
        d64[hb * 64:(hb + 1) * 64, 1, hb * 64:(hb + 1) * 64] = np.sin(a) / 8.0
    return dict(cmask=cmask, rope=rope, retla=retla, dftL=dft(TL), dftC=dft(TC), dft64=d64)


def host_layout(inp, b, consts):
    m = dict(consts)
    m["x"] = np.ascontiguousarray(inp["x"][b], dtype=np.float32)
    m["ctx"] = np.ascontiguousarray(inp["ctx"][b], dtype=np.float32)
    cT = np.zeros((128, 8, 2), np.float32)
    cT[:, :, 0] = _fm_cols(inp["c"][b])
    cT[:, :, 1] = _fm_cols(inp["c_ctx"])
    m["cT"] = cT
    pv = np.zeros((NLAYER, 128, NPV), np.float32)
    tmv = np.zeros((NLAYER, 128, 4, 256), np.float32)
    for l in range(NLAYER):
        def put(name, arr):
            o, c = PV[name]
            pv[l, :, o:o + c] = arr
        put("b_ada", _fm_cols(inp["b_ada"][l]))
        put("g1", _fm_cols(inp["g_norm1"][l]))
        put("g2", _fm_cols(inp["g_norm2"][l]))
        put("mu", np.concatenate([_fm_cols(inp["rwkv_mu"][l][j]) for j in range(3)], axis=1))
        put("gla_gn", _fm_cols(inp["gla_gn"][l]))
        put("ret_gn", _fm_cols(inp["ret_gn"][l]))
        put("rconv", np.concatenate([_fm_cols(inp["rwkv_conv"][l][j]) for j in range(3)], axis=1))
        put("a0", np.concatenate([_fm_cols(inp["rwkv_a0"][l][j]) for j in range(2)], axis=1))
        put("kkw", _fm_cols(inp["rwkv_kk"][l]))
        put("ka", _fm_cols(inp["rwkv_ka"][l]))
        put("rk", _fm_cols(inp["rwkv_rk"][l]))
        put("rwkv_gn", _fm_cols(inp["rwkv_gn"][l]))
        put("b_gate", np.concatenate([_fm_cols(inp["b_gate"][l][j]) for j in range(4)], axis=1))
        put("fconv", np.concatenate([_fm_cols(inp["ffn_conv"][l][j]) for j in range(3)], axis=1))
        put("fconvb", _fm_cols(inp["ffn_conv_b"][l]))
        for j in range(2):
            tmv[l, :, j, :] = np.asarray(inp["gla_ba"][l][j], np.float32)[None, :]
            tmv[l, :, 2 + j, :] = np.asarray(inp["rwkv_w0"][l][j], np.float32)[None, :]
    m["pv"] = pv
    m["tmv"] = tmv
    m["gfin"] = np.ascontiguousarray(np.broadcast_to(np.asarray(inp["g_final"], np.float32)[None, :], (128, D)))
    for n in WEIGHT_NAMES:
        m[n] = np.ascontiguousarray(inp[n], dtype=np.float32)
    return m


_CACHE = {}


def kernel(**inputs):
    TL = inputs["x"].shape[1]
    B = inputs["x"].shape[0]
    key = ("nc", TL)
    if key not in _CACHE:
        _CACHE[key] = (build(Cfg(TL=TL)), host_consts(TL))
    nc, consts = _CACHE[key]
    in_maps = [host_layout(inputs, b, consts) for b in range(B)]
    res = run_bass_kernel_spmd(nc, in_maps, core_ids=list(range(B)))
    return np.stack([np.asarray(r["out"], np.float32) for r in res.results], axis=0)
```
